# Optimizing a Trainium2 kernel written in Bass

```python
import jax, jax.numpy as jnp
from jax import lax
import numpy as np

D_MODEL = 1024
BATCH = 16
SEQ = 2048
DEPTH = 1

D_MIX = 2 * D_MODEL
D_SSD = D_MIX // 2
SSD_HEAD_DIM = 64
SSD_HEADS = D_SSD // SSD_HEAD_DIM
SSD_GROUPS = 2
HEADS_PER_GROUP = SSD_HEADS // SSD_GROUPS
D_STATE = 128
SSD_CONV = 4
CHUNK = 128
D_XBC = D_SSD + 2 * SSD_GROUPS * D_STATE
D_CF = D_MIX - D_SSD
CF_KERNEL = 31
D_FF = 4 * D_MODEL
D_IN_PROJ = D_SSD + D_XBC + SSD_HEADS + 2 * D_CF
EPS = 1e-5

kernel_name = "hymba_style_ssd_conformer_hybrid"


def rmsnorm(x, w):
    xf = x.astype(jnp.float32)
    y = xf * lax.rsqrt(jnp.mean(xf * xf, axis=-1, keepdims=True) + EPS)
    return (y * w.astype(jnp.float32)).astype(x.dtype)


def layernorm(x, w, b):
    xf = x.astype(jnp.float32)
    mu = jnp.mean(xf, axis=-1, keepdims=True)
    var = jnp.mean(jnp.square(xf - mu), axis=-1, keepdims=True)
    y = (xf - mu) * lax.rsqrt(var + EPS)
    return (y * w.astype(jnp.float32) + b.astype(jnp.float32)).astype(x.dtype)


def gated_rmsnorm(y, z, w):
    g = (y * jax.nn.silu(z)).astype(jnp.float32)
    shp = g.shape
    g = g.reshape(shp[:-1] + (SSD_GROUPS, shp[-1] // SSD_GROUPS))
    g = g * lax.rsqrt(jnp.mean(g * g, axis=-1, keepdims=True) + EPS)
    return (g.reshape(shp) * w.astype(jnp.float32)).astype(y.dtype)


def causal_dwconv(u, w, b):
    k = w.shape[0]
    out = lax.conv_general_dilated(
        u, w[:, None, :].astype(u.dtype), window_strides=(1,), padding=[(k - 1, 0)],
        dimension_numbers=("NWC", "WIO", "NWC"), feature_group_count=u.shape[-1])
    return out + b.astype(u.dtype)


def ssd_chunked(xh, dt, a, bm, cm):
    b, s, _, p = xh.shape
    nc = s // CHUNK
    xr = (xh.astype(jnp.float32) * dt[..., None]).reshape(b, nc, CHUNK, SSD_GROUPS, HEADS_PER_GROUP, p)
    la = (dt * a).reshape(b, nc, CHUNK, SSD_GROUPS, HEADS_PER_GROUP)
    la = jnp.moveaxis(la, 2, -1)
    a_cs = jnp.cumsum(la, axis=-1)
    br = bm.astype(jnp.float32).reshape(b, nc, CHUNK, SSD_GROUPS, D_STATE)
    cr = cm.astype(jnp.float32).reshape(b, nc, CHUNK, SSD_GROUPS, D_STATE)
    causal = jnp.tril(jnp.ones((CHUNK, CHUNK), dtype=bool))
    seg = a_cs[..., :, None] - a_cs[..., None, :]
    decay = jnp.exp(jnp.where(causal, seg, -jnp.inf))
    cb = jnp.einsum("bclgn,bcsgn->bcgls", cr, br)
    y_diag = jnp.einsum("bcgls,bcghls,bcsghp->bclghp", cb, decay, xr)
    decay_st = jnp.exp(a_cs[..., -1:] - a_cs)
    states = jnp.einsum("bclgn,bcghl,bclghp->bcghpn", br, decay_st, xr)
    chunk_decay = jnp.exp(a_cs[..., -1])

    def step(hstate, inp):
        st, dc = inp
        return hstate * dc[..., None, None] + st, hstate

    h0 = jnp.zeros((b, SSD_GROUPS, HEADS_PER_GROUP, p, D_STATE), jnp.float32)
    _, prev = lax.scan(step, h0, (jnp.moveaxis(states, 1, 0), jnp.moveaxis(chunk_decay, 1, 0)))
    prev = jnp.moveaxis(prev, 0, 1)
    y_off = jnp.einsum("bclgn,bcghpn,bcghl->bclghp", cr, prev, jnp.exp(a_cs))
    return (y_diag + y_off).reshape(b, s, SSD_HEADS, p)


def hybrid_mixer(h, w_in, conv_ssd_w, conv_ssd_b, dt_bias, a_log, d_skip, ssd_norm_w,
                 conv_cf_w, conv_cf_b, cf_ln_w, cf_ln_b, w_out):
    b, s, _ = h.shape
    proj = h @ w_in.astype(h.dtype)
    i1 = D_SSD
    i2 = i1 + D_XBC
    i3 = i2 + SSD_HEADS
    i4 = i3 + D_CF
    z, xbc, dt_raw, glu_a, glu_b = jnp.split(proj, [i1, i2, i3, i4], axis=-1)
    xbc = jax.nn.silu(causal_dwconv(xbc, conv_ssd_w, conv_ssd_b))
    xs, bm, cm = jnp.split(xbc, [D_SSD, D_SSD + SSD_GROUPS * D_STATE], axis=-1)
    dt = jax.nn.softplus(dt_raw.astype(jnp.float32) + dt_bias.astype(jnp.float32))
    a = -jnp.exp(a_log.astype(jnp.float32))
    xh = xs.reshape(b, s, SSD_HEADS, SSD_HEAD_DIM)
    bm = bm.reshape(b, s, SSD_GROUPS, D_STATE)
    cm = cm.reshape(b, s, SSD_GROUPS, D_STATE)
    y = ssd_chunked(xh, dt, a, bm, cm) + d_skip.astype(jnp.float32)[:, None] * xh.astype(jnp.float32)
    y = gated_rmsnorm(y.reshape(b, s, D_SSD).astype(h.dtype), z, ssd_norm_w)
    u = glu_a * jax.nn.sigmoid(glu_b)
    u = causal_dwconv(u, conv_cf_w, conv_cf_b)
    u = jax.nn.silu(layernorm(u, cf_ln_w, cf_ln_b))
    return jnp.concatenate([y, u], axis=-1) @ w_out.astype(h.dtype)


def setup_inputs(seed: int = 0) -> dict:
    key = jax.random.key(seed)
    ks = jax.random.split(key, 24)
    nrm = lambda k, shp, sc: jax.random.normal(k, shp, jnp.float32) * sc
    L = DEPTH
    x = jax.random.normal(ks[0], (BATCH, SEQ, D_MODEL), jnp.float32)
    c = jax.random.normal(ks[1], (BATCH, D_MODEL), jnp.float32)
    w_ada = nrm(ks[2], (L, D_MODEL, 6 * D_MODEL), D_MODEL ** -0.5)
    b_ada = nrm(ks[3], (L, 6 * D_MODEL), 0.02)
    norm_mix_w = 1.0 + nrm(ks[4], (L, D_MODEL), 0.02)
    w_in = nrm(ks[5], (L, D_MODEL, D_IN_PROJ), D_MODEL ** -0.5)
    conv_ssd_w = nrm(ks[6], (L, SSD_CONV, D_XBC), SSD_CONV ** -0.5)
    conv_ssd_b = nrm(ks[7], (L, D_XBC), 0.02)
    dt0 = jnp.exp(jax.random.uniform(ks[8], (L, SSD_HEADS), jnp.float32,
                                     np.log(1e-3).astype(np.float32), np.log(1e-1).astype(np.float32)))
    dt_bias = dt0 + jnp.log(-jnp.expm1(-dt0))
    a_log = jnp.log(jax.random.uniform(ks[9], (L, SSD_HEADS), jnp.float32, 1.0, 16.0))
    d_skip = 1.0 + nrm(ks[10], (L, SSD_HEADS), 0.1)
    ssd_norm_w = 1.0 + nrm(ks[11], (L, D_SSD), 0.02)
    conv_cf_w = nrm(ks[12], (L, CF_KERNEL, D_CF), CF_KERNEL ** -0.5)
    conv_cf_b = nrm(ks[13], (L, D_CF), 0.02)
    cf_ln_w = 1.0 + nrm(ks[14], (L, D_CF), 0.02)
    cf_ln_b = nrm(ks[15], (L, D_CF), 0.02)
    w_out = nrm(ks[16], (L, D_MIX, D_MODEL), D_MIX ** -0.5)
    norm_mlp_w = 1.0 + nrm(ks[17], (L, D_MODEL), 0.02)
    w_mlp1 = nrm(ks[18], (L, D_MODEL, D_FF), D_MODEL ** -0.5)
    w_mlp2 = nrm(ks[19], (L, D_FF, D_MODEL), D_FF ** -0.5)
    norm_final_w = 1.0 + nrm(ks[20], (D_MODEL,), 0.02)
    return {"x": x, "c": c, "w_ada": w_ada, "b_ada": b_ada, "norm_mix_w": norm_mix_w,
            "w_in": w_in, "conv_ssd_w": conv_ssd_w, "conv_ssd_b": conv_ssd_b,
            "dt_bias": dt_bias, "a_log": a_log, "d_skip": d_skip, "ssd_norm_w": ssd_norm_w,
            "conv_cf_w": conv_cf_w, "conv_cf_b": conv_cf_b, "cf_ln_w": cf_ln_w, "cf_ln_b": cf_ln_b,
            "w_out": w_out, "norm_mlp_w": norm_mlp_w, "w_mlp1": w_mlp1, "w_mlp2": w_mlp2,
            "norm_final_w": norm_final_w}


def reference(x, c, w_ada, b_ada, norm_mix_w, w_in, conv_ssd_w, conv_ssd_b, dt_bias, a_log,
              d_skip, ssd_norm_w, conv_cf_w, conv_cf_b, cf_ln_w, cf_ln_b, w_out,
              norm_mlp_w, w_mlp1, w_mlp2, norm_final_w):
    c_act = jax.nn.silu(c)
    for l in range(DEPTH):
        mod = (c_act @ w_ada[l].astype(c.dtype) + b_ada[l].astype(c.dtype))[:, None, :]
        sh_mix, sc_mix, g_mix, sh_mlp, sc_mlp, g_mlp = jnp.split(mod, 6, axis=-1)
        h = rmsnorm(x, norm_mix_w[l]) * (1.0 + sc_mix) + sh_mix
        x = x + g_mix * hybrid_mixer(h, w_in[l], conv_ssd_w[l], conv_ssd_b[l], dt_bias[l], a_log[l],
                                     d_skip[l], ssd_norm_w[l], conv_cf_w[l], conv_cf_b[l],
                                     cf_ln_w[l], cf_ln_b[l], w_out[l])
        h = rmsnorm(x, norm_mlp_w[l]) * (1.0 + sc_mlp) + sh_mlp
        x = x + g_mlp * (jnp.square(jax.nn.relu(h @ w_mlp1[l].astype(h.dtype))) @ w_mlp2[l].astype(h.dtype))
    return rmsnorm(x, norm_final_w)
```

```python
import numpy as np
import concourse.bass as bass
import concourse.mybir as mybir
from concourse.bass_utils import run_bass_kernel_spmd

F32 = mybir.dt.float32
BF16 = mybir.dt.bfloat16
AF = mybir.ActivationFunctionType
ALU = mybir.AluOpType

D = 1024
SEQ = 2048
NB = 16
NCORES = 8
D_IN = 4624
EPS = 1e-5
SBT = 1024
NRING = 8
TB3 = 512


class Buf:
    __slots__ = ("name", "w", "rs", "aliases", "space", "lo", "hi")

    def __init__(self, reg, name, space=None, lo=0, hi=0):
        self.name = name
        self.w = None
        self.rs = []
        self.aliases = []
        self.space = space
        self.lo = lo
        self.hi = hi
        if space is not None:
            for o in reg:
                if o.space == space and o.lo < hi and lo < o.hi:
                    o.aliases.append(self)
                    self.aliases.append(o)
            reg.append(self)


class T:
    __slots__ = ("ap", "bufs")

    def __init__(self, ap, bufs):
        self.ap = ap
        self.bufs = bufs if isinstance(bufs, (list, tuple)) else [bufs]

    def __getitem__(self, key):
        return T(self.ap[key], self.bufs)

    def v(self, ap):
        return T(ap, self.bufs)


class Op:
    __slots__ = ("stream", "issue", "seq", "fn", "deps", "sig", "semval", "waits", "clock", "is_dma", "nd")

    def __init__(self):
        self.sig = False
        self.semval = 0
        self.waits = []
        self.clock = None


COMPUTE = ("pe", "act", "dve", "pool", "sp")


class Sched:
    def __init__(self):
        self.ops = []
        self.count = {}
        self.by_stream = {}
        self.last_dma = {}

    def _deps(self, reads, writes):
        deps = []
        for t in reads:
            for b in t.bufs:
                if b.w is not None:
                    deps.append(b.w)
                for a in b.aliases:
                    if a.w is not None:
                        deps.append(a.w)
        for t in writes:
            for b in t.bufs:
                for bb in [b] + b.aliases:
                    if bb.w is not None:
                        deps.append(bb.w)
                    deps.extend(bb.rs)
        return deps

    def _commit(self, op, reads, writes):
        for t in writes:
            for b in t.bufs:
                b.w = op
                b.rs = []
        for t in reads:
            for b in t.bufs:
                b.rs.append(op)

    def op(self, eng, fn, reads=(), writes=()):
        o = Op()
        o.stream = eng
        o.issue = eng
        o.is_dma = False
        o.fn = fn
        o.deps = self._deps(reads, writes)
        self.count[eng] = self.count.get(eng, 0) + 1
        o.seq = self.count[eng]
        self._commit(o, reads, writes)
        self.ops.append(o)
        self.by_stream.setdefault(eng, []).append(o)
        return o

    def dma(self, queue, sem_name, fns, reads=(), writes=()):
        o = Op()
        o.stream = "dma:" + sem_name
        o.issue = queue
        o.is_dma = True
        o.fn = fns
        o.nd = len(fns)
        o.deps = self._deps(reads, writes)
        prev = self.last_dma.get(o.stream)
        if prev is not None:
            o.deps.append(prev)
        self.last_dma[o.stream] = o
        self.count[o.stream] = self.count.get(o.stream, 0) + 1
        o.seq = self.count[o.stream]
        self._commit(o, reads, writes)
        self.ops.append(o)
        self.by_stream.setdefault(o.stream, []).append(o)
        return o

    def plan(self):
        known = {e: {} for e in COMPUTE}
        for o in self.ops:
            k = known[o.issue]
            need = {}
            for d in o.deps:
                if d.stream == "pe" and o.stream == "pe":
                    continue
                if k.get(d.stream, 0) >= d.seq:
                    continue
                if need.get(d.stream, 0) < d.seq:
                    need[d.stream] = d.seq
            for s, q in need.items():
                if k.get(s, 0) >= q:
                    continue
                d = self.by_stream[s][q - 1]
                d.sig = True
                o.waits.append(d)
                for s2, q2 in d.clock.items():
                    if k.get(s2, 0) < q2:
                        k[s2] = q2
            o.clock = dict(k)
            o.clock[o.stream] = o.seq
        for s, lst in self.by_stream.items():
            c = 0
            for o in lst:
                if o.is_dma:
                    c += 16 * o.nd
                    o.semval = c
                elif o.sig:
                    c += 1
                    o.semval = c

    def emit(self, block, sems):
        per = {e: [o for o in self.ops if o.issue == e] for e in COMPUTE}

        def run(eng, lst):
            for o in lst:
                ws = [(sems[d.stream], d.semval) for d in o.waits]
                if o.is_dma:
                    for (s, v) in ws:
                        eng.wait_ge(s, v)
                    for f in o.fn:
                        f(eng).then_inc(sems[o.stream], 16)
                else:
                    for (s, v) in ws[1:]:
                        eng.wait_ge(s, v)
                    ins = o.fn(eng)
                    if ws:
                        ins._wait_ge(ws[0][0], ws[0][1])
                    if o.sig:
                        ins.then_inc(sems[o.stream], 1)

        @block.tensor
        def _(e):
            run(e, per["pe"])

        @block.scalar
        def _(e):
            run(e, per["act"])

        @block.vector
        def _(e):
            run(e, per["dve"])

        @block.gpsimd
        def _(e):
            run(e, per["pool"])

        @block.sync
        def _(e):
            run(e, per["sp"])


C_NMW, C_NLW, C_SNW, C_LNW, C_LNB, C_CFB, C_CSB, C_CSW, C_CFW = 0, 8, 16, 24, 32, 40, 48, 60, 108
NCST = 108 + 8 * 31
R_DTB, R_ALOG, R_DSK, R_WFIN = 0, 16, 32, 48
NROW = 48 + 1024


def _fm(v):
    return np.ascontiguousarray(v.reshape(-1, 128).T)


def pack_consts(inp):
    cst = np.zeros((128, NCST), np.float32)
    cst[:, C_NMW:C_NMW + 8] = _fm(inp["norm_mix_w"][0])
    cst[:, C_NLW:C_NLW + 8] = _fm(inp["norm_mlp_w"][0])
    cst[:, C_SNW:C_SNW + 8] = _fm(inp["ssd_norm_w"][0])
    cst[:, C_LNW:C_LNW + 8] = _fm(inp["cf_ln_w"][0])
    cst[:, C_LNB:C_LNB + 8] = _fm(inp["cf_ln_b"][0])
    cst[:, C_CFB:C_CFB + 8] = _fm(inp["conv_cf_b"][0])
    cst[:, C_CSB:C_CSB + 12] = _fm(inp["conv_ssd_b"][0])
    w = inp["conv_ssd_w"][0]
    cst[:, C_CSW:C_CSW + 48] = w.reshape(4, 12, 128).transpose(2, 1, 0).reshape(128, 48)
    w = inp["conv_cf_w"][0]
    cst[:, C_CFW:C_CFW + 248] = w.reshape(31, 8, 128).transpose(2, 1, 0).reshape(128, 248)
    rows = np.zeros((128, NROW), np.float32)
    rows[:, R_DTB:R_DTB + 16] = inp["dt_bias"][0][None, :]
    rows[:, R_ALOG:R_ALOG + 16] = inp["a_log"][0][None, :]
    rows[:, R_DSK:R_DSK + 16] = inp["d_skip"][0][None, :]
    rows[:, R_WFIN:R_WFIN + 1024] = inp["norm_final_w"][None, :]
    return cst, rows


def build_nc(nsb=4, dbg=False):
    nc = bass.Bass("TRN2", target_bir_lowering=False)
    nseq = (nsb + 1) // 2
    x_d = nc.dram_tensor("x", [2, SEQ, D], F32, kind="ExternalInput").ap()
    cT_d = nc.dram_tensor("cT", [D, 2], F32, kind="ExternalInput").ap()
    wada_d = nc.dram_tensor("w_ada", [D, 6 * D], F32, kind="ExternalInput").ap()
    bada_d = nc.dram_tensor("b_ada", [1, 6 * D], F32, kind="ExternalInput").ap()
    win_d = nc.dram_tensor("w_in", [D, D_IN], F32, kind="ExternalInput").ap()
    wout_d = nc.dram_tensor("w_out", [2 * D, D], F32, kind="ExternalInput").ap()
    w1_d = nc.dram_tensor("w_mlp1", [D, 4 * D], F32, kind="ExternalInput").ap()
    w2_d = nc.dram_tensor("w_mlp2", [4 * D, D], F32, kind="ExternalInput").ap()
    cst_d = nc.dram_tensor("cst", [128, NCST], F32, kind="ExternalInput").ap()
    rows_d = nc.dram_tensor("rows", [128, NROW], F32, kind="ExternalInput").ap()
    out_d = nc.dram_tensor("out", [2, SEQ, D], F32, kind="ExternalOutput").ap()
    dbg_d = {}
    if dbg:
        for nm in ("dbgA", "dbgB", "dbgC"):
            dbg_d[nm] = nc.dram_tensor(nm, [SBT, D], F32, kind="ExternalOutput").ap()
        dbg_d["dbgH"] = nc.dram_tensor("dbgH", [128, 8 * SBT], BF16, kind="ExternalOutput").ap()

    S = Sched()
    reg = []
    NW = 53200
    sem_names = ["pe", "act", "dve", "pool", "dcst", "dada", "dout0", "dout1", "ddbg", "dwf", "dwdt"] + \
        ["dx%d" % i for i in range(8)] + ["dr%d" % i for i in range(NRING)]
    import contextlib
    with contextlib.ExitStack() as es:
        arena = es.enter_context(nc.sbuf_tensor("arena", [128, NW], F32))
        ps = es.enter_context(nc.psum_tensor("ps", [128, 4096], F32))
        sems = {}
        for nm in sem_names:
            h = es.enter_context(nc.semaphore("s_" + nm))
            sems[nm if nm in COMPUTE else "dma:" + nm] = h
        block = es.enter_context(nc.Block())

        off = [0]

        def alloc(name, n, dt=F32, at=None):
            nw = n if dt == F32 else (n + 1) // 2
            if at is None:
                lo = off[0]
                off[0] += nw
                assert off[0] <= NW, (name, off[0])
            else:
                lo = at
            ap = arena[:, lo:lo + nw]
            if dt != F32:
                ap = ap.bitcast(dt)
            return T(ap, Buf(reg, name, "sb", lo, lo + nw)), lo

        def A(name, n, dt=F32):
            return alloc(name, n, dt)[0]

        pbank = [T(ps[:, 512 * i:512 * (i + 1)], Buf(reg, "pb%d" % i)) for i in range(8)]
        prr = [0]
        nrot = [3]

        def ppair():
            i = prr[0] % nrot[0]
            prr[0] = (i + 1) % nrot[0]
            t = T(ps[:, 1024 * i:1024 * (i + 1)], [pbank[2 * i].bufs[0], pbank[2 * i + 1].bufs[0]])
            return t, pbank[2 * i], pbank[2 * i + 1]

        plong = (pbank[6], pbank[7])

        X = A("X", 8 * 1024)
        Xv = X.ap.rearrange("p (t d) -> p t d", t=8)
        Xt = []
        for t in range(8):
            lo = X.bufs[0].lo + t * 1024
            Xt.append(T(Xv[:, t, :], Buf(reg, "X%d" % t, "sb", lo, lo + 1024)))
        grep = [A("grep%d" % w, 1024) for w in range(2)]
        cst = A("cst", NCST)
        rows = A("rows", 48)
        modv = A("modv", 96)
        gwv = A("gwv", 32)
        ident = A("ident", 128)
        identb = A("identb", 128, BF16)
        triU = A("triU", 128)
        mstr = A("mstr", 128)
        onesf = A("onesf", 128)
        onesD = A("onesD", 128)
        arow = A("arow", 16)
        wdt = A("wdt", 8 * 16, BF16)
        Sst = A("Sst", 1024)
        Sb = A("Sb", 1024, BF16)
        halo_s = A("halo_s", 36, BF16)
        ucf = []
        for c in range(8):
            ut_, lo_ = alloc("ucf%d" % c, 30 + TB3, BF16)
            ucf.append((ut_.ap, Buf(reg, "ucfh%d" % c), Buf(reg, "ucfb%d" % c)))
        ssq = A("ssq", 16)
        idq = A("idq", 32, BF16)
        dgc = A("dgc", 248 * 32, BF16)
        dgs = A("dgs", 48 * 32, BF16)
        dgc3 = dgc.ap.rearrange("p (m j) -> p m j", j=32)
        dgs3 = dgs.ap.rearrange("p (m j) -> p m j", j=32)
        hT, hT_lo = alloc("hT", 8 * SBT, BF16)
        hTv = hT.ap.rearrange("p (k t) -> p k t", k=8)
        hTg = [Buf(reg, "hTg%d" % g) for g in range(4)]

        def hTs(k, t0, n):
            return T(hTv[:, k, t0:t0 + n], [hTg[g] for g in range(t0 // 256, (t0 + n - 1) // 256 + 1)])
        ring = []
        for i in range(NRING):
            ring.append(A("ring%d" % i, 4096, BF16))
        work_lo = off[0]
        assert hT_lo + 8 * 6144 // 2 <= NW
        off[0] = hT_lo + 8 * 6144 // 2
        cTs = A("cTs", 16)
        cact = A("cact", 8 * 33, BF16)
        mrow = A("mrow", 512)
        off[0] = work_lo

        def mm(out, lhsT, rhs, start, stop):
            S.op("pe", lambda e: e.matmul(out.ap, lhsT.ap, rhs.ap, start=start, stop=stop), reads=[lhsT, rhs], writes=[out])

        def mmq(out, lhsT, rhs, start, stop, i):
            S.op("pe", lambda e: e.matmul(out.ap, lhsT.ap, rhs.ap, start=start, stop=stop, tile_position=(32 * i, 32 * i)),
                 reads=[lhsT, rhs], writes=[out])

        def tr(out, in_, idn):
            S.op("pe", lambda e: e.transpose(out=out.ap, in_=in_.ap, identity=idn.ap), reads=[in_, idn], writes=[out])

        def act(out, in_, func, bias=None, scale=None, accum=None, extra_r=(), extra_w=()):
            kw = {}
            rd = [in_] + list(extra_r)
            wr = [out] + list(extra_w)
            if bias is not None:
                if isinstance(bias, T):
                    kw["bias"] = bias.ap
                    rd.append(bias)
                else:
                    kw["bias"] = bias
            if scale is not None:
                if isinstance(scale, T):
                    kw["scale"] = scale.ap
                    rd.append(scale)
                else:
                    kw["scale"] = scale
            if accum is not None:
                kw["accum_out"] = accum.ap
                wr.append(accum)
            S.op("act", lambda e: e.activation(out=out.ap, in_=in_.ap, func=func, **kw), reads=rd, writes=wr)

        def tt(out, in0, in1, op, eng="dve"):
            S.op(eng, lambda e: e.tensor_tensor(out=out.ap, in0=in0.ap, in1=in1.ap, op=op), reads=[in0, in1], writes=[out])

        def ts(out, in0, s1, s2, op0, op1=None, eng="dve"):
            rd = [in0]
            a1 = s1.ap if isinstance(s1, T) else s1
            a2 = s2.ap if isinstance(s2, T) else s2
            if isinstance(s1, T):
                rd.append(s1)
            if isinstance(s2, T):
                rd.append(s2)
            if op1 is None:
                assert op0 == ALU.mult
                S.op(eng, lambda e: e.tensor_scalar_mul(out=out.ap, in0=in0.ap, scalar1=a1), reads=rd, writes=[out])
            else:
                S.op(eng, lambda e: e.tensor_scalar(out=out.ap, in0=in0.ap, scalar1=a1, scalar2=a2, op0=op0, op1=op1), reads=rd, writes=[out])

        def stt(out, in0, sc, in1, op0, op1, eng="dve"):
            rd = [in0, in1]
            a = sc.ap if isinstance(sc, T) else sc
            if isinstance(sc, T):
                rd.append(sc)
            S.op(eng, lambda e: e.scalar_tensor_tensor(out=out.ap, in0=in0.ap, scalar=a, in1=in1.ap, op0=op0, op1=op1), reads=rd, writes=[out])

        def cp(out, in_, eng="dve"):
            S.op(eng, lambda e: e.tensor_copy(out=out.ap, in_=in_.ap), reads=[in_], writes=[out])

        def memset(out, val, eng="dve"):
            S.op(eng, lambda e: e.memset(out.ap, val), writes=[out])

        def recip(out, in_):
            S.op("dve", lambda e: e.reciprocal(out=out.ap, in_=in_.ap), reads=[in_], writes=[out])

        wada, _ = alloc("wada", 8 * 6144, BF16, at=hT_lo)
        wadav = wada.ap.rearrange("p (k c) -> p k c", k=8)
        for gb_ in hTg:
            gb_.aliases.append(wada.bufs[0])
            wada.bufs[0].aliases.append(gb_)
        brow, _ = alloc("brow", 6144, F32, at=X.bufs[0].lo)
        cactv = cact.ap.rearrange("p (k m) -> p k m", k=8)
        memset(brow[0:33, :], 0.0)
        S.dma("sp", "dcst", [
            lambda e: e.dma_start(out=cst.ap, in_=cst_d[:, :]),
            lambda e: e.dma_start(out=rows.ap, in_=rows_d[:, 0:48]),
            lambda e: e.dma_start(out=cTs.ap.rearrange("p (k b) -> p k b", k=8), in_=cT_d.rearrange("(k p) b -> p k b", p=128)),
            lambda e: e.dma_start(out=brow.ap[0:1, :], in_=bada_d[0:1, :]),
            lambda e: e.dma_start(out=brow.ap[32:33, :], in_=bada_d[0:1, :]),
        ], writes=[cst, rows, cTs, brow])
        S.dma("pool", "dada", [
            (lambda e, j=j: e.dma_start(out=wadav[:, :, j * 1024:(j + 1) * 1024],
                                        in_=wada_d[:, j * 1024:(j + 1) * 1024].rearrange("(k p) c -> p k c", p=128)))
            for j in range(6)], writes=[wada])
        S.dma("pool", "dwdt", [
            lambda e: e.dma_start(out=wdt.ap.rearrange("p (k c) -> p k c", k=8),
                                  in_=win_d[:, 2560:2576].rearrange("(k p) c -> p k c", p=128))], writes=[wdt])
        memset(ident, 0.0, "pool")
        S.op("pool", lambda e: e.affine_select(out=ident.ap, in_=ident.ap, pattern=[[-1, 128]], compare_op=ALU.not_equal,
                                               fill=1.0, base=0, channel_multiplier=1), reads=[ident], writes=[ident])
        cp(identb, ident)
        memset(triU, 1.0, "pool")
        S.op("pool", lambda e: e.affine_select(out=triU.ap, in_=triU.ap, pattern=[[1, 128]], compare_op=ALU.is_ge,
                                               fill=0.0, base=0, channel_multiplier=-1), reads=[triU], writes=[triU])
        memset(mstr, 1.0, "pool")
        S.op("pool", lambda e: e.affine_select(out=mstr.ap, in_=mstr.ap, pattern=[[-1, 128]], compare_op=ALU.is_ge,
                                               fill=0.0, base=-1, channel_multiplier=1), reads=[mstr], writes=[mstr])
        for i in range(4):
            cp(idq[32 * i:32 * i + 32, :], identb[32 * i:32 * i + 32, 32 * i:32 * i + 32])
        tt(dgc.v(dgc3), idq.v(idq.ap.unsqueeze(1).to_broadcast([128, 248, 32])),
           cst.v(cst.ap[:, C_CFW:C_CFW + 248].unsqueeze(2).to_broadcast([128, 248, 32])), ALU.mult)
        tt(dgs.v(dgs3), idq.v(idq.ap.unsqueeze(1).to_broadcast([128, 48, 32])),
           cst.v(cst.ap[:, C_CSW:C_CSW + 48].unsqueeze(2).to_broadcast([128, 48, 32])), ALU.mult)
        memset(onesf, 1.0)
        memset(onesD, 1.0 / D)
        memset(cact, 0.0)
        cTv = cTs.ap.rearrange("p (k b) -> p k b", k=8)
        for b in range(2):
            act(cact.v(cactv[:, :, 32 * b:32 * b + 1]), cTs.v(cTv[:, :, b:b + 1]), AF.Silu)
        act(arow, rows[:, R_ALOG:R_ALOG + 16], AF.Exp)
        ts(arow, arow, -1.0, None, ALU.mult)
        for j in range(12):
            pp, pa, pb_ = ppair()
            for k in range(8):
                mm(pa[0:33, :], cact.v(cactv[:, k, :]), wada.v(wadav[:, k, j * 512:(j + 1) * 512]), k == 0, k == 7)
            tt(mrow[0:33, :], pa[0:33, :], brow[0:33, j * 512:(j + 1) * 512], ALU.add)
            vec = {0: 0, 1: 0, 2: 1, 3: 1, 4: 4, 5: 4, 6: 2, 7: 2, 8: 3, 9: 3, 10: 5, 11: 5}[j]
            for b in range(2):
                for q in range(4):
                    mm(pb_[:, b * 4 + q:b * 4 + q + 1], mrow[32 * b:32 * b + 1, q * 128:(q + 1) * 128], onesf[32 * b:32 * b + 1, 0:1], True, True)
                col = b * 48 + vec * 8 + (j % 2) * 4
                cp(modv[:, col:col + 4], pb_[:, b * 4:b * 4 + 4])
        for b in range(2):
            for wch in range(2):
                sc_ = modv[:, b * 48 + (1 + 2 * wch) * 8: b * 48 + (1 + 2 * wch) * 8 + 8]
                nw_ = cst[:, (C_NMW if wch == 0 else C_NLW):(C_NMW if wch == 0 else C_NLW) + 8]
                stt(gwv[:, b * 16 + wch * 8:b * 16 + wch * 8 + 8], sc_, 1.0, nw_, ALU.add, ALU.mult)

        def wreset():
            off[0] = work_lo

        cnt = {"ring": 0, "ost": 0}

        def load_piece(src_fn):
            i = cnt["ring"] % NRING
            cnt["ring"] += 1
            slot = ring[i]
            S.dma("pool", "dr%d" % i, src_fn(slot), writes=[slot])
            return slot

        def piece_cols(w_d, c0, ncol):
            def f(slot):
                v = slot.ap[:, 0:8 * ncol].rearrange("p (k c) -> p k c", k=8)
                return [lambda e: e.dma_start(out=v, in_=w_d[:, c0:c0 + ncol].rearrange("(k p) c -> p k c", p=128))]
            return f

        def piece_rows(w_d, r0, nk):
            def f(slot):
                v = slot.ap[:, 0:nk * 1024].rearrange("p (k c) -> p k c", k=nk)
                return [lambda e: e.dma_start(out=v, in_=w_d[r0:r0 + nk * 128, :].rearrange("(k p) c -> p k c", p=128))]
            return f

        def cols_view(slot, ncol):
            return slot.ap[:, 0:8 * ncol].rearrange("p (k c) -> p k c", k=8)

        def rows_view(slot, nk):
            return slot.ap[:, 0:nk * 1024].rearrange("p (k c) -> p k c", k=nk)

        def rmsnorm_to_hT(b, wch, xn_bufs, junk):
            for t in range(8):
                act(junk, Xt[t], AF.Square, accum=ssq[:, t:t + 1])
            act(ssq[:, 8:16], ssq[:, 0:8], AF.Sqrt, bias=EPS, scale=1.0 / D)
            recip(ssq[:, 8:16], ssq[:, 8:16])
            for g in range(4):
                pairs = [ppair(), ppair()]
                for j in range(2):
                    t = 2 * g + j
                    xn = xn_bufs[t % 2]
                    ts(xn, Xt[t], ssq[:, 8 + t:9 + t], None, ALU.mult)
                    for k in range(8):
                        pp = pairs[k // 4][1 + (k % 4) // 2]
                        c0 = ((k % 4) % 2) * 256 + j * 128
                        tr(pp[:, c0:c0 + 128], xn[:, k * 128:(k + 1) * 128], ident)
                for k in range(8):
                    pp = pairs[k // 4][1 + (k % 4) // 2]
                    c0 = ((k % 4) % 2) * 256
                    act(hTs(k, g * 256, 256), pp[:, c0:c0 + 256], AF.Identity,
                        bias=modv[:, b * 48 + (2 * wch) * 8 + k: b * 48 + (2 * wch) * 8 + k + 1],
                        scale=gwv[:, b * 16 + wch * 8 + k: b * 16 + wch * 8 + k + 1])

        def resid_add(t, pp, grp, tmp):
            tt(tmp, pp, grp, ALU.mult)
            tt(Xt[t], Xt[t], tmp, ALU.add)

        for sb in range(nsb):
            b = sb // 2
            first = (sb % 2 == 0)
            tokbase = (sb % 2) * SBT
            for t in range(8):
                S.dma("sp", "dx%d" % t, [(lambda e, t=t, b=b, tokbase=tokbase: e.dma_start(out=Xt[t].ap, in_=x_d[b, tokbase + t * 128: tokbase + (t + 1) * 128, :]))],
                      writes=[Xt[t]])
            xb_p = [load_piece(piece_cols(win_d, 1024 + 512 * i, 512)) for i in range(3)]
            z_p = [load_piece(piece_cols(win_d, 512 * i, 512)) for i in range(2)]
            wos_p = [load_piece(piece_rows(wout_d, 512 * i, 4)) for i in range(2)]

            wreset()
            xn_bufs = [A("xn0", 1024), A("xn1", 1024)]
            junk = A("junk", 1024, BF16)
            rmsnorm_to_hT(b, 0, xn_bufs, junk)
            if dbg and sb == 0:
                S.dma("sp", "ddbg", [lambda e: e.dma_start(out=dbg_d["dbgH"][:, :], in_=hT.ap)], reads=[T(None, hTg)])

            wreset()
            xpre = [A("xpre%d" % i, 260, BF16) for i in range(2)]
            xbcT = [A("xbcT%d" % c, 256, BF16) for c in range(12)]
            xh2 = [A("xh%d" % i, 1024, BF16) for i in range(2)]
            Bt2 = [A("Bt%d" % i, 256, BF16) for i in range(2)]
            esb2 = [A("esb%d" % i, 48) for i in range(2)]
            xw2 = [A("xw%d" % i, 1024, BF16) for i in range(2)]
            yb2 = [A("yb%d" % i, 1024) for i in range(2)]
            dtt = A("dtt", 32)
            lat = A("lat", 32)
            acsb = A("acsb", 48)
            dtw = A("dtw", 16)
            rla = [A("rla%d" % i, 512) for i in range(2)]
            Eb = [A("E%d" % i, 512, BF16) for i in range(2)]
            MT = A("MT", 2048, BF16)
            cbm = A("cbm", 256)
            xr = A("xr", 1024, BF16)
            tF = A("tF", 1024)
            tB = A("tB", 1024)
            ycT = [A("ycT%d" % k, 128, BF16) for k in range(8)]
            ss2 = A("ss2", 4)
            dgt = tF[:, 0:128]
            if first:
                memset(Sst, 0.0)
                memset(Sb, 0.0)
                memset(halo_s, 0.0)
                for c in range(8):
                    memset(T(ucf[c][0][:, 0:30], ucf[c][1]), 0.0)
                for which in range(2):
                    for hk in range(2):
                        pp, pa, pb_ = ppair()
                        for q in range(4):
                            k = hk * 4 + q
                            col = b * 48 + (4 + which) * 8 + k
                            ts(dgt, ident, modv[:, col:col + 1], None, ALU.mult)
                            mm(pa[:, q * 128:(q + 1) * 128], onesf, dgt, True, True)
                        cp(grep[which][:, hk * 512:(hk + 1) * 512], pa[:, :])

            def X_stage(blk):
                tok0 = blk * 256
                pp, pa, pb_ = ppair()
                wdv = wdt.ap.rearrange("p (k c) -> p k c", k=8)
                for j in range(2):
                    for k in range(8):
                        mm(pa[:, j * 16:(j + 1) * 16], hTs(k, tok0 + j * 128, 128), wdt.v(wdv[:, k, :]), k == 0, k == 7)
                for j in range(2):
                    tt(dtt[:, j * 16:(j + 1) * 16], pa[:, j * 16:(j + 1) * 16], rows[:, R_DTB:R_DTB + 16], ALU.add)
                act(dtt, dtt, AF.Exp)
                act(dtt, dtt, AF.Ln, bias=1.0)
                for j in range(2):
                    tt(lat[:, j * 16:(j + 1) * 16], dtt[:, j * 16:(j + 1) * 16], arow, ALU.mult)
                yield

                for c0 in range(0, 12, 2):
                    prs = []
                    for c in (c0, c0 + 1):
                        pp, pa, pb_ = ppair()
                        prs.append((pa, pb_))
                        wv = cols_view(xb_p[c // 4], 512)
                        for k in range(8):
                            mm(pa[:, 0:256], xb_p[c // 4].v(wv[:, k, (c % 4) * 128:(c % 4 + 1) * 128]), hTs(k, tok0, 256), k == 0, k == 7)
                    for c, (pa, pb_) in zip((c0, c0 + 1), prs):
                        xp = xpre[c % 2]
                        cp(xp[:, 0:3], halo_s[:, c * 3:c * 3 + 3])
                        act(xp[:, 3:259], pa[:, 0:256], AF.Copy)
                        cp(halo_s[:, c * 3:c * 3 + 3], xp[:, 256:259])
                    for c, (pa, pb_) in zip((c0, c0 + 1), prs):
                        xp = xpre[c % 2]
                        for k in range(4):
                            for i in range(4):
                                mmq(pb_[32 * i:32 * i + 32, 0:256], dgs.v(dgs3[32 * i:32 * i + 32, c * 4 + k, :]),
                                    xp[32 * i:32 * i + 32, k:k + 256], k == 0, k == 3, i)
                        act(xbcT[c], pb_[:, 0:256], AF.Silu, bias=cst[:, C_CSB + c:C_CSB + c + 1])
                    yield
            def F_stage(g):
                j = g % 2
                g2 = g % 2
                xh, Bt, esb, xw, yb = xh2[g2], Bt2[g2], esb2[g2], xw2[g2], yb2[g2]
                dt_j = dtt[:, j * 16:(j + 1) * 16]
                la_j = lat[:, j * 16:(j + 1) * 16]
                pp, pa, pb_ = ppair()
                pab = pa.v(pa.ap.bitcast(BF16))
                pbb = pb_.v(pb_.ap.bitcast(BF16))
                for c in range(8):
                    tr(pab[:, c * 128:(c + 1) * 128], xbcT[c][:, j * 128:(j + 1) * 128], identb)
                for gg in range(2):
                    tr(pbb[:, gg * 128:(gg + 1) * 128], xbcT[8 + gg][:, j * 128:(j + 1) * 128], identb)
                cp(xh, pab[:, 0:1024])
                cp(Bt, pbb[:, 0:256])
                yield
                pq, pqa, pqb = ppair()
                mm(pqa[:, 0:16], triU, la_j, True, True)
                mm(pqa[:, 16:32], onesf, la_j, True, True)
                for gg in range(2):
                    mm(pqb[:, gg * 128:(gg + 1) * 128], xbcT[8 + gg][:, j * 128:(j + 1) * 128], xbcT[10 + gg][:, j * 128:(j + 1) * 128], True, True)
                act(acsb[:, 0:32], pqa[:, 0:32], AF.Copy)
                tt(acsb[:, 32:48], acsb[:, 16:32], acsb[:, 0:16], ALU.subtract)
                act(esb, acsb, AF.Exp)
                tt(dtw, dt_j, esb[:, 32:48], ALU.mult)
                cbv = cbm.ap.rearrange("p (g l) -> p g l", g=2)
                tt(cbm.v(cbv), pqb.v(pqb.ap[:, 0:256].rearrange("p (g l) -> p g l", g=2)),
                   triU.v(triU.ap.unsqueeze(1).to_broadcast([128, 2, 128])), ALU.mult)
                xh3 = xh.v(xh.ap.rearrange("p (h q) -> p h q", h=16))
                tt(xr.v(xr.ap.rearrange("p (h q) -> p h q", h=16)), xh3, dt_j.v(dt_j.ap.unsqueeze(2).to_broadcast([128, 16, 64])), ALU.mult)
                tt(xw.v(xw.ap.rearrange("p (h q) -> p h q", h=16)), xh3, dtw.v(dtw.ap.unsqueeze(2).to_broadcast([128, 16, 64])), ALU.mult)
                yield
                MTv = MT.ap.rearrange("p (h l) -> p h l", h=16)
                def mk_rla(q4):
                    rl = rla[q4 % 2]
                    rl3 = rl.v(rl.ap.rearrange("p (h l) -> p h l", h=4))
                    la4 = la_j[:, q4 * 4:(q4 + 1) * 4]
                    tt(rl3, triU.v(triU.ap.unsqueeze(1).to_broadcast([128, 4, 128])),
                       la4.v(la4.ap.unsqueeze(2).to_broadcast([128, 4, 128])), ALU.mult, eng="pool")

                mk_rla(0)
                mk_rla(1)
                for q4 in range(4):
                    rl = rla[q4 % 2]
                    Eq = Eb[q4 % 2]
                    pg, pga, pgb = ppair()
                    mm(pga[:, :], mstr, rl, True, True)
                    if q4 < 2:
                        mk_rla(q4 + 2)
                    act(Eq, pga, AF.Exp)
                    gg = q4 // 2
                    tt(MT.v(MTv[:, q4 * 4:(q4 + 1) * 4, :]), Eq.v(Eq.ap.rearrange("p (h l) -> p h l", h=4)),
                       cbm.v(cbv[:, gg:gg + 1, :].to_broadcast([128, 4, 128])), ALU.mult)
                    yield
                tt(tF.v(tF.ap.rearrange("p (h q) -> p h q", h=16)), xh3,
                   rows.v(rows.ap[:, R_DSK:R_DSK + 16].unsqueeze(2).to_broadcast([128, 16, 64])), ALU.mult)
                pd, pda, pdb = ppair()
                xrv = xr.ap.rearrange("p (h q) -> p h q", h=16)
                for h in range(16):
                    pdd = pda if h < 8 else pdb
                    mm(pdd[:, (h % 8) * 64:(h % 8 + 1) * 64], MT.v(MTv[:, h, :]), xr.v(xrv[:, h, :]), True, True)
                tt(yb, pd, tF, ALU.add)
                yield

            def S_stage(g):
                j = g % 2
                g2 = g % 2
                Bt, esb, xw, yb = Bt2[g2], esb2[g2], xw2[g2], yb2[g2]
                po, poa, pob = ppair()
                Sbv = Sb.ap.rearrange("p (g m) -> p g m", g=2)
                for gg, pgg in enumerate((poa, pob)):
                    mm(pgg[:, :], xbcT[10 + gg][:, j * 128:(j + 1) * 128], Sb.v(Sbv[:, gg, :]), True, True)
                tt(tB.v(tB.ap.rearrange("p (h q) -> p h q", h=16)), po.v(po.ap.rearrange("p (h q) -> p h q", h=16)),
                   esb.v(esb.ap[:, 0:16].unsqueeze(2).to_broadcast([128, 16, 64])), ALU.mult)
                tt(yb, yb, tB, ALU.add)
                pst, psa, psb = ppair()
                Btv = Bt.ap.rearrange("p (g n) -> p g n", g=2)
                xwv = xw.ap.rearrange("p (g m) -> p g m", g=2)
                for gg, pgg in enumerate((psa, psb)):
                    mm(pgg[:, :], Bt.v(Btv[:, gg, :]), xw.v(xwv[:, gg, :]), True, True)
                S3 = Sst.v(Sst.ap.rearrange("p (h q) -> p h q", h=16))
                tt(S3, S3, esb.v(esb.ap[:, 16:32].unsqueeze(2).to_broadcast([128, 16, 64])), ALU.mult)
                tt(Sst, Sst, pst, ALU.add)
                act(Sb, Sst, AF.Copy)

            def B_stage(g):
                t = g
                tk = g * 128
                yb = yb2[g % 2]
                zs = tB
                gnb = tB
                pz, pza, pzb = ppair()
                for half, pzz in enumerate((pza, pzb)):
                    zv = cols_view(z_p[half], 512)
                    for k in range(8):
                        mm(pzz[:, :], hTs(k, tk, 128), z_p[half].v(zv[:, k, :]), k == 0, k == 7)
                act(zs, pz, AF.Silu)
                tt(yb, yb, zs, ALU.mult)
                yield
                pj, pja, pjb = ppair()
                for gg, pjj in enumerate((pja, pjb)):
                    act(pjj[:, :], yb[:, gg * 512:(gg + 1) * 512], AF.Square, accum=ss2[:, gg:gg + 1])
                act(ss2[:, 2:4], ss2[:, 0:2], AF.Sqrt, bias=EPS, scale=1.0 / 512)
                recip(ss2[:, 2:4], ss2[:, 2:4])
                for gg in range(2):
                    ts(gnb[:, gg * 512:(gg + 1) * 512], yb[:, gg * 512:(gg + 1) * 512], ss2[:, 2 + gg:3 + gg], None, ALU.mult)
                yield
                pt, pta, ptb = ppair()
                for k in range(8):
                    tr(pt[:, k * 128:(k + 1) * 128], gnb[:, k * 128:(k + 1) * 128], ident)
                for k in range(8):
                    act(ycT[k], pt[:, k * 128:(k + 1) * 128], AF.Copy, scale=cst[:, C_SNW + k:C_SNW + k + 1])
                yield
                px, pxa, pxb = ppair()
                for half, pxx in enumerate((pxa, pxb)):
                    for k in range(8):
                        wv = rows_view(wos_p[k // 4], 4)
                        mm(pxx[:, :], ycT[k], wos_p[k // 4].v(wv[:, k % 4, half * 512:(half + 1) * 512]), k == 0, k == 7)
                resid_add(t, px, grep[0], yb)
                yield

            def chain(*gens):
                for gn in gens:
                    yield from gn

            def merge(ga, na, gb, nb):
                ia = ib = 0
                da = db = False
                while not (da and db):
                    ta = (ia + 1) * nb if not da else None
                    tb_ = (ib + 1) * na if not db else None
                    pick_a = (not da) and (db or ta <= tb_)
                    if pick_a:
                        try:
                            next(ga)
                            ia += 1
                        except StopIteration:
                            da = True
                    else:
                        try:
                            next(gb)
                            ib += 1
                        except StopIteration:
                            db = True

            for _ in X_stage(0):
                pass
            for _ in F_stage(0):
                pass
            def SB_stage(g):
                S_stage(g)
                yield
                yield from B_stage(g)

            p3_pieces = None
            for g in range(8):
                if g == 6:
                    gb0 = load_piece(piece_cols(win_d, 3600, 512))
                    ga0 = load_piece(piece_cols(win_d, 2576, 512))
                    gb1 = load_piece(piece_cols(win_d, 3600 + 512, 512))
                    ga1 = load_piece(piece_cols(win_d, 2576 + 512, 512))
                    p3_pieces = ([ga0, ga1], [gb0, gb1])
                front = []
                nf = 0
                if g % 2 == 1 and g < 7:
                    front.append(X_stage((g + 1) // 2))
                    nf += 7
                if g < 7:
                    front.append(F_stage(g + 1))
                    nf += 8
                if front:
                    merge(SB_stage(g), 5, chain(*front), nf)
                else:
                    for _ in SB_stage(g):
                        pass
            if dbg and sb == 0:
                S.dma("sp", "ddbg", [lambda e: e.dma_start(out=dbg_d["dbgA"].rearrange("(t p) d -> p t d", p=128), in_=Xv)], reads=[X])

            ga_p, gb_p = p3_pieces
            woc_p = [load_piece(piece_rows(wout_d, 1024 + 512 * i, 4)) for i in range(2)]
            wreset()
            sig = [A("sig%d" % i, TB3) for i in range(2)]
            vv = [A("v%d" % c, TB3) for c in range(8)]
            vsq = [A("vsq%d" % i, TB3) for i in range(2)]
            mean_sb = A("mean_sb", TB3)
            rstd_sb = A("rstd_sb", TB3)
            tln = [A("tln%d" % i, TB3) for i in range(2)]
            ucT = [A("ucT%d" % c, TB3, BF16) for c in range(8)]
            tmpx = A("tmpx3", 1024)
            for blk in range(SBT // TB3):
                tok0 = blk * TB3
                psta, pstb = plong
                nrot[0] = 3

                def glu(c):
                    pp, pa, pb_ = ppair()
                    gbv = cols_view(gb_p[c // 4], 512)
                    gav = cols_view(ga_p[c // 4], 512)
                    for k in range(8):
                        mm(pa[:, 0:TB3], gb_p[c // 4].v(gbv[:, k, (c % 4) * 128:(c % 4 + 1) * 128]), hTs(k, tok0, TB3), k == 0, k == 7)
                    for k in range(8):
                        mm(pb_[:, 0:TB3], ga_p[c // 4].v(gav[:, k, (c % 4) * 128:(c % 4 + 1) * 128]), hTs(k, tok0, TB3), k == 0, k == 7)
                    return pa, pb_

                def stats(c):
                    mm(psta[:, 0:TB3], onesD, vv[c], c == 0, c == 7)
                    mm(pstb[:, 0:TB3], onesD, vsq[c % 2], c == 0, c == 7)

                nxt = glu(0)
                for c in range(8):
                    pa, pb_ = nxt
                    sg = sig[c % 2]
                    act(sg, pa[:, 0:TB3], AF.Sigmoid)
                    uap, uh, ub = ucf[c]
                    tt(T(uap[:, 30:30 + TB3], ub), pb_[:, 0:TB3], sg, ALU.mult)
                    if c < 7:
                        nxt = glu(c + 1)
                    if c > 0:
                        stats(c - 1)
                    v = vv[c]
                    pp2, pc, pc2 = ppair()
                    for k in range(31):
                        for i in range(4):
                            mmq(pc[32 * i:32 * i + 32, 0:TB3], dgc.v(dgc3[32 * i:32 * i + 32, c * 31 + k, :]),
                                T(uap[32 * i:32 * i + 32, k:k + TB3], [uh, ub]), k == 0, k == 30, i)
                    act(v, pc[:, 0:TB3], AF.Identity, bias=cst[:, C_CFB + c:C_CFB + c + 1])
                    act(vsq[c % 2], v, AF.Square)
                stats(7)
                for c in range(8):
                    uap, uh, ub = ucf[c]
                    cp(T(uap[:, 0:30], uh), T(uap[:, TB3:TB3 + 30], ub))
                act(mean_sb, psta[:, 0:TB3], AF.Copy)
                tt(rstd_sb, mean_sb, mean_sb, ALU.mult)
                tt(rstd_sb, pstb[:, 0:TB3], rstd_sb, ALU.subtract)
                act(rstd_sb, rstd_sb, AF.Sqrt, bias=EPS)
                recip(rstd_sb, rstd_sb)
                for c in range(8):
                    tl = tln[c % 2]
                    tt(tl, vv[c], mean_sb, ALU.subtract)
                    tt(tl, tl, rstd_sb, ALU.mult)
                    act(ucT[c], tl, AF.Silu, bias=cst[:, C_LNB + c:C_LNB + c + 1], scale=cst[:, C_LNW + c:C_LNW + c + 1])
                for j in range(TB3 // 128):
                    t = blk * (TB3 // 128) + j
                    px, pxa, pxb = ppair()
                    for half, pxx in enumerate((pxa, pxb)):
                        for k in range(8):
                            wv = rows_view(woc_p[k // 4], 4)
                            mm(pxx[:, :], ucT[k][:, j * 128:(j + 1) * 128], woc_p[k // 4].v(wv[:, k % 4, half * 512:(half + 1) * 512]), k == 0, k == 7)
                    resid_add(t, px, grep[0], tmpx)
            if dbg and sb == 0:
                S.dma("sp", "ddbg", [lambda e: e.dma_start(out=dbg_d["dbgB"].rearrange("(t p) d -> p t d", p=128), in_=Xv)], reads=[X])

            nrot[0] = 3
            wreset()
            xn_bufs = [A("xn0b", 1024), A("xn1b", 1024)]
            junk = A("junkb", 1024, BF16)
            rr = [A("rr%d" % i, 512) for i in range(2)]
            hid = [A("hid%d" % f, 512, BF16) for f in range(8)]
            tmpx = A("tmpx4", 1024)
            rmsnorm_to_hT(b, 1, xn_bufs, junk)
            for q4 in range(4):
                w1p = [load_piece(piece_cols(w1_d, 1024 * q4 + 512 * i, 512)) for i in range(2)]
                w2p = [load_piece(piece_rows(w2_d, 1024 * q4 + 512 * i, 4)) for i in range(2)]
                for blk in range(2):
                    for f in range(8):
                        pp, pa, pb_ = ppair()
                        w1v = cols_view(w1p[f // 4], 512)
                        for k in range(8):
                            mm(pa[:, :], w1p[f // 4].v(w1v[:, k, (f % 4) * 128:(f % 4 + 1) * 128]), hTs(k, blk * 512, 512), k == 0, k == 7)
                        r = rr[f % 2]
                        act(r, pa, AF.Relu)
                        tt(hid[f], r, r, ALU.mult)
                    for j in range(4):
                        t = blk * 4 + j
                        px, pxa, pxb = ppair()
                        for half, pxx in enumerate((pxa, pxb)):
                            for f in range(8):
                                w2v = rows_view(w2p[f // 4], 4)
                                mm(pxx[:, :], hid[f][:, j * 128:(j + 1) * 128], w2p[f // 4].v(w2v[:, f % 4, half * 512:(half + 1) * 512]), f == 0, f == 7)
                        resid_add(t, px, grep[1], tmpx)
            if dbg and sb == 0:
                S.dma("sp", "ddbg", [lambda e: e.dma_start(out=dbg_d["dbgC"].rearrange("(t p) d -> p t d", p=128), in_=Xv)], reads=[X])

            wreset()
            junk = A("junkf", 1024, BF16)
            ost = [A("ost%d" % i, 1024) for i in range(2)]
            wfin = A("wfin", 1024)
            S.dma("sp", "dwf", [lambda e: e.dma_start(out=wfin.ap, in_=rows_d[:, R_WFIN:R_WFIN + 1024])], writes=[wfin])
            for t in range(8):
                act(junk, Xt[t], AF.Square, accum=ssq[:, t:t + 1])
            act(ssq[:, 8:16], ssq[:, 0:8], AF.Sqrt, bias=EPS, scale=1.0 / D)
            recip(ssq[:, 8:16], ssq[:, 8:16])
            for t in range(8):
                i = cnt["ost"] % 2
                cnt["ost"] += 1
                stt(ost[i], Xt[t], ssq[:, 8 + t:9 + t], wfin, ALU.mult, ALU.mult)
                S.dma("sp", "dout%d" % i, [(lambda e, t=t, i=i, b=b, tokbase=tokbase, o=ost[i]: e.dma_start(out=out_d[b, tokbase + t * 128: tokbase + (t + 1) * 128, :], in_=o.ap))],
                      reads=[ost[i]])

        fin = S.op("sp", lambda e: e.nop())
        for nm in ("dma:dout0", "dma:dout1", "dma:ddbg"):
            if nm in S.last_dma:
                fin.deps.append(S.last_dma[nm])
        S.plan()
        S.emit(block, sems)
    return nc


_NC_CACHE = {}


def kernel(**inputs):
    inp = {k: np.asarray(v) for k, v in inputs.items()}
    if "nc" not in _NC_CACHE:
        _NC_CACHE["nc"] = build_nc()
    nc = _NC_CACHE["nc"]
    cst, rows = pack_consts(inp)
    shared = {
        "w_ada": np.ascontiguousarray(inp["w_ada"][0]), "b_ada": np.ascontiguousarray(inp["b_ada"][0:1]),
        "w_in": np.ascontiguousarray(inp["w_in"][0]), "w_out": np.ascontiguousarray(inp["w_out"][0]),
        "w_mlp1": np.ascontiguousarray(inp["w_mlp1"][0]), "w_mlp2": np.ascontiguousarray(inp["w_mlp2"][0]),
        "cst": cst, "rows": rows,
    }
    in_maps = []
    for i in range(NCORES):
        m = dict(shared)
        m["x"] = np.ascontiguousarray(inp["x"][2 * i:2 * i + 2])
        m["cT"] = np.ascontiguousarray(inp["c"][2 * i:2 * i + 2].T)
        in_maps.append(m)
    res = run_bass_kernel_spmd(nc, in_maps, core_ids=list(range(NCORES)))
    return np.concatenate([r["out"] for r in res.results], axis=0).astype(np.float32)
```

```python
import numpy as np
import concourse.bass as bass
import concourse.mybir as mybir
from concourse.bass_utils import run_bass_kernel_spmd

F32 = mybir.dt.float32
BF16 = mybir.dt.bfloat16
AF = mybir.ActivationFunctionType
ALU = mybir.AluOpType

D = 1024
SEQ = 2048
NB = 16
NCORES = 8
D_IN = 4624
EPS = 1e-5
SBT = 1024
NRING = 8
TB3 = 512


class Buf:
    __slots__ = ("name", "w", "rs", "aliases", "space", "lo", "hi")

    def __init__(self, reg, name, space=None, lo=0, hi=0):
        self.name = name
        self.w = None
        self.rs = []
        self.aliases = []
        self.space = space
        self.lo = lo
        self.hi = hi
        if space is not None:
            for o in reg:
                if o.space == space and o.lo < hi and lo < o.hi:
                    o.aliases.append(self)
                    self.aliases.append(o)
            reg.append(self)


class T:
    __slots__ = ("ap", "bufs")

    def __init__(self, ap, bufs):
        self.ap = ap
        self.bufs = bufs if isinstance(bufs, (list, tuple)) else [bufs]

    def __getitem__(self, key):
        return T(self.ap[key], self.bufs)

    def v(self, ap):
        return T(ap, self.bufs)


class Op:
    __slots__ = ("stream", "issue", "seq", "fn", "deps", "sig", "semval", "waits", "clock", "is_dma", "nd")

    def __init__(self):
        self.sig = False
        self.semval = 0
        self.waits = []
        self.clock = None


COMPUTE = ("pe", "act", "dve", "pool", "sp")


class Sched:
    def __init__(self):
        self.ops = []
        self.count = {}
        self.by_stream = {}
        self.last_dma = {}

    def _deps(self, reads, writes):
        deps = []
        for t in reads:
            for b in t.bufs:
                if b.w is not None:
                    deps.append(b.w)
                for a in b.aliases:
                    if a.w is not None:
                        deps.append(a.w)
        for t in writes:
            for b in t.bufs:
                for bb in [b] + b.aliases:
                    if bb.w is not None:
                        deps.append(bb.w)
                    deps.extend(bb.rs)
        return deps

    def _commit(self, op, reads, writes):
        for t in writes:
            for b in t.bufs:
                b.w = op
                b.rs = []
        for t in reads:
            for b in t.bufs:
                b.rs.append(op)

    def op(self, eng, fn, reads=(), writes=()):
        o = Op()
        o.stream = eng
        o.issue = eng
        o.is_dma = False
        o.fn = fn
        o.deps = self._deps(reads, writes)
        self.count[eng] = self.count.get(eng, 0) + 1
        o.seq = self.count[eng]
        self._commit(o, reads, writes)
        self.ops.append(o)
        self.by_stream.setdefault(eng, []).append(o)
        return o

    def dma(self, queue, sem_name, fns, reads=(), writes=()):
        o = Op()
        o.stream = "dma:" + sem_name
        o.issue = queue
        o.is_dma = True
        o.fn = fns
        o.nd = len(fns)
        o.deps = self._deps(reads, writes)
        prev = self.last_dma.get(o.stream)
        if prev is not None:
            o.deps.append(prev)
        self.last_dma[o.stream] = o
        self.count[o.stream] = self.count.get(o.stream, 0) + 1
        o.seq = self.count[o.stream]
        self._commit(o, reads, writes)
        self.ops.append(o)
        self.by_stream.setdefault(o.stream, []).append(o)
        return o

    def plan(self):
        known = {e: {} for e in COMPUTE}
        for o in self.ops:
            k = known[o.issue]
            need = {}
            for d in o.deps:
                if d.stream == "pe" and o.stream == "pe":
                    continue
                if k.get(d.stream, 0) >= d.seq:
                    continue
                if need.get(d.stream, 0) < d.seq:
                    need[d.stream] = d.seq
            for s, q in need.items():
                if k.get(s, 0) >= q:
                    continue
                d = self.by_stream[s][q - 1]
                d.sig = True
                o.waits.append(d)
                for s2, q2 in d.clock.items():
                    if k.get(s2, 0) < q2:
                        k[s2] = q2
            o.clock = dict(k)
            o.clock[o.stream] = o.seq
        for s, lst in self.by_stream.items():
            c = 0
            for o in lst:
                if o.is_dma:
                    c += 16 * o.nd
                    o.semval = c
                elif o.sig:
                    c += 1
                    o.semval = c

    def emit(self, block, sems):
        per = {e: [o for o in self.ops if o.issue == e] for e in COMPUTE}

        def run(eng, lst):
            for o in lst:
                ws = [(sems[d.stream], d.semval) for d in o.waits]
                if o.is_dma:
                    for (s, v) in ws:
                        eng.wait_ge(s, v)
                    for f in o.fn:
                        f(eng).then_inc(sems[o.stream], 16)
                else:
                    for (s, v) in ws[1:]:
                        eng.wait_ge(s, v)
                    ins = o.fn(eng)
                    if ws:
                        ins._wait_ge(ws[0][0], ws[0][1])
                    if o.sig:
                        ins.then_inc(sems[o.stream], 1)

        @block.tensor
        def _(e):
            run(e, per["pe"])

        @block.scalar
        def _(e):
            run(e, per["act"])

        @block.vector
        def _(e):
            run(e, per["dve"])

        @block.gpsimd
        def _(e):
            run(e, per["pool"])

        @block.sync
        def _(e):
            run(e, per["sp"])


C_NMW, C_NLW, C_SNW, C_LNW, C_LNB, C_CFB, C_CSB, C_CSW, C_CFW = 0, 8, 16, 24, 32, 40, 48, 60, 108
NCST = 108 + 8 * 31
R_DTB, R_ALOG, R_DSK, R_WFIN = 0, 16, 32, 48
NROW = 48 + 1024


def _fm(v):
    return np.ascontiguousarray(v.reshape(-1, 128).T)


def pack_consts(inp):
    cst = np.zeros((128, NCST), np.float32)
    cst[:, C_NMW:C_NMW + 8] = _fm(inp["norm_mix_w"][0])
    cst[:, C_NLW:C_NLW + 8] = _fm(inp["norm_mlp_w"][0])
    cst[:, C_SNW:C_SNW + 8] = _fm(inp["ssd_norm_w"][0])
    cst[:, C_LNW:C_LNW + 8] = _fm(inp["cf_ln_w"][0])
    cst[:, C_LNB:C_LNB + 8] = _fm(inp["cf_ln_b"][0])
    cst[:, C_CFB:C_CFB + 8] = _fm(inp["conv_cf_b"][0])
    cst[:, C_CSB:C_CSB + 12] = _fm(inp["conv_ssd_b"][0])
    w = inp["conv_ssd_w"][0]
    cst[:, C_CSW:C_CSW + 48] = w.reshape(4, 12, 128).transpose(2, 1, 0).reshape(128, 48)
    w = inp["conv_cf_w"][0]
    cst[:, C_CFW:C_CFW + 248] = w.reshape(31, 8, 128).transpose(2, 1, 0).reshape(128, 248)
    rows = np.zeros((128, NROW), np.float32)
    rows[:, R_DTB:R_DTB + 16] = inp["dt_bias"][0][None, :]
    rows[:, R_ALOG:R_ALOG + 16] = inp["a_log"][0][None, :]
    rows[:, R_DSK:R_DSK + 16] = inp["d_skip"][0][None, :]
    rows[:, R_WFIN:R_WFIN + 1024] = inp["norm_final_w"][None, :]
    return cst, rows


def build_nc(nsb=4, dbg=False):
    nc = bass.Bass("TRN2", target_bir_lowering=False)
    nseq = (nsb + 1) // 2
    x_d = nc.dram_tensor("x", [2, SEQ, D], F32, kind="ExternalInput").ap()
    cT_d = nc.dram_tensor("cT", [D, 2], F32, kind="ExternalInput").ap()
    wada_d = nc.dram_tensor("w_ada", [D, 6 * D], F32, kind="ExternalInput").ap()
    bada_d = nc.dram_tensor("b_ada", [1, 6 * D], F32, kind="ExternalInput").ap()
    win_d = nc.dram_tensor("w_in", [D, D_IN], F32, kind="ExternalInput").ap()
    wout_d = nc.dram_tensor("w_out", [2 * D, D], F32, kind="ExternalInput").ap()
    w1_d = nc.dram_tensor("w_mlp1", [D, 4 * D], F32, kind="ExternalInput").ap()
    w2_d = nc.dram_tensor("w_mlp2", [4 * D, D], F32, kind="ExternalInput").ap()
    cst_d = nc.dram_tensor("cst", [128, NCST], F32, kind="ExternalInput").ap()
    rows_d = nc.dram_tensor("rows", [128, NROW], F32, kind="ExternalInput").ap()
    out_d = nc.dram_tensor("out", [2, SEQ, D], F32, kind="ExternalOutput").ap()
    dbg_d = {}
    if dbg:
        for nm in ("dbgA", "dbgB", "dbgC"):
            dbg_d[nm] = nc.dram_tensor(nm, [SBT, D], F32, kind="ExternalOutput").ap()
        dbg_d["dbgH"] = nc.dram_tensor("dbgH", [128, 8 * SBT], BF16, kind="ExternalOutput").ap()

    S = Sched()
    reg = []
    NW = 53200
    sem_names = ["pe", "act", "dve", "pool", "dcst", "dada", "dout0", "dout1", "ddbg", "dwf", "dwdt"] + \
        ["dx%d" % i for i in range(8)] + ["dr%d" % i for i in range(NRING)]
    import contextlib
    with contextlib.ExitStack() as es:
        arena = es.enter_context(nc.sbuf_tensor("arena", [128, NW], F32))
        ps = es.enter_context(nc.psum_tensor("ps", [128, 4096], F32))
        sems = {}
        for nm in sem_names:
            h = es.enter_context(nc.semaphore("s_" + nm))
            sems[nm if nm in COMPUTE else "dma:" + nm] = h
        block = es.enter_context(nc.Block())

        off = [0]

        def alloc(name, n, dt=F32, at=None):
            nw = n if dt == F32 else (n + 1) // 2
            if at is None:
                lo = off[0]
                off[0] += nw
                assert off[0] <= NW, (name, off[0])
            else:
                lo = at
            ap = arena[:, lo:lo + nw]
            if dt != F32:
                ap = ap.bitcast(dt)
            return T(ap, Buf(reg, name, "sb", lo, lo + nw)), lo

        def A(name, n, dt=F32):
            return alloc(name, n, dt)[0]

        pbank = [T(ps[:, 512 * i:512 * (i + 1)], Buf(reg, "pb%d" % i)) for i in range(8)]
        prr = [0]
        nrot = [3]

        def ppair():
            i = prr[0] % nrot[0]
            prr[0] = (i + 1) % nrot[0]
            t = T(ps[:, 1024 * i:1024 * (i + 1)], [pbank[2 * i].bufs[0], pbank[2 * i + 1].bufs[0]])
            return t, pbank[2 * i], pbank[2 * i + 1]

        plong = (pbank[6], pbank[7])

        X = A("X", 8 * 1024)
        Xv = X.ap.rearrange("p (t d) -> p t d", t=8)
        Xt = []
        for t in range(8):
            lo = X.bufs[0].lo + t * 1024
            Xt.append(T(Xv[:, t, :], Buf(reg, "X%d" % t, "sb", lo, lo + 1024)))
        grep = [A("grep%d" % w, 1024) for w in range(2)]
        cst = A("cst", NCST)
        rows = A("rows", 48)
        modv = A("modv", 96)
        gwv = A("gwv", 32)
        ident = A("ident", 128)
        identb = A("identb", 128, BF16)
        triU = A("triU", 128)
        mstr = A("mstr", 128)
        onesf = A("onesf", 128)
        onesD = A("onesD", 128)
        arow = A("arow", 16)
        wdt = A("wdt", 8 * 16, BF16)
        Sst = A("Sst", 1024)
        Sb = A("Sb", 1024, BF16)
        halo_s = A("halo_s", 36, BF16)
        ucf = []
        for c in range(8):
            ut_, lo_ = alloc("ucf%d" % c, 30 + TB3, BF16)
            ucf.append((ut_.ap, Buf(reg, "ucfh%d" % c), Buf(reg, "ucfb%d" % c)))
        ssq = A("ssq", 16)
        idq = A("idq", 32, BF16)
        dgc = A("dgc", 248 * 32, BF16)
        dgs = A("dgs", 48 * 32, BF16)
        dgc3 = dgc.ap.rearrange("p (m j) -> p m j", j=32)
        dgs3 = dgs.ap.rearrange("p (m j) -> p m j", j=32)
        hT, hT_lo = alloc("hT", 8 * SBT, BF16)
        hTv = hT.ap.rearrange("p (k t) -> p k t", k=8)
        hTg = [Buf(reg, "hTg%d" % g) for g in range(4)]

        def hTs(k, t0, n):
            return T(hTv[:, k, t0:t0 + n], [hTg[g] for g in range(t0 // 256, (t0 + n - 1) // 256 + 1)])
        ring = []
        for i in range(NRING):
            ring.append(A("ring%d" % i, 4096, BF16))
        work_lo = off[0]
        assert hT_lo + 8 * 6144 // 2 <= NW
        off[0] = hT_lo + 8 * 6144 // 2
        cTs = A("cTs", 16)
        cact = A("cact", 8 * 33, BF16)
        mrow = A("mrow", 512)
        off[0] = work_lo

        def mm(out, lhsT, rhs, start, stop):
            S.op("pe", lambda e: e.matmul(out.ap, lhsT.ap, rhs.ap, start=start, stop=stop), reads=[lhsT, rhs], writes=[out])

        def mmq(out, lhsT, rhs, start, stop, i):
            S.op("pe", lambda e: e.matmul(out.ap, lhsT.ap, rhs.ap, start=start, stop=stop, tile_position=(32 * i, 32 * i)),
                 reads=[lhsT, rhs], writes=[out])

        def tr(out, in_, idn):
            S.op("pe", lambda e: e.transpose(out=out.ap, in_=in_.ap, identity=idn.ap), reads=[in_, idn], writes=[out])

        def act(out, in_, func, bias=None, scale=None, accum=None, extra_r=(), extra_w=()):
            kw = {}
            rd = [in_] + list(extra_r)
            wr = [out] + list(extra_w)
            if bias is not None:
                if isinstance(bias, T):
                    kw["bias"] = bias.ap
                    rd.append(bias)
                else:
                    kw["bias"] = bias
            if scale is not None:
                if isinstance(scale, T):
                    kw["scale"] = scale.ap
                    rd.append(scale)
                else:
                    kw["scale"] = scale
            if accum is not None:
                kw["accum_out"] = accum.ap
                wr.append(accum)
            S.op("act", lambda e: e.activation(out=out.ap, in_=in_.ap, func=func, **kw), reads=rd, writes=wr)

        def tt(out, in0, in1, op, eng="dve"):
            S.op(eng, lambda e: e.tensor_tensor(out=out.ap, in0=in0.ap, in1=in1.ap, op=op), reads=[in0, in1], writes=[out])

        def ts(out, in0, s1, s2, op0, op1=None, eng="dve"):
            rd = [in0]
            a1 = s1.ap if isinstance(s1, T) else s1
            a2 = s2.ap if isinstance(s2, T) else s2
            if isinstance(s1, T):
                rd.append(s1)
            if isinstance(s2, T):
                rd.append(s2)
            if op1 is None:
                assert op0 == ALU.mult
                S.op(eng, lambda e: e.tensor_scalar_mul(out=out.ap, in0=in0.ap, scalar1=a1), reads=rd, writes=[out])
            else:
                S.op(eng, lambda e: e.tensor_scalar(out=out.ap, in0=in0.ap, scalar1=a1, scalar2=a2, op0=op0, op1=op1), reads=rd, writes=[out])

        def stt(out, in0, sc, in1, op0, op1, eng="dve"):
            rd = [in0, in1]
            a = sc.ap if isinstance(sc, T) else sc
            if isinstance(sc, T):
                rd.append(sc)
            S.op(eng, lambda e: e.scalar_tensor_tensor(out=out.ap, in0=in0.ap, scalar=a, in1=in1.ap, op0=op0, op1=op1), reads=rd, writes=[out])

        def cp(out, in_, eng="dve"):
            S.op(eng, lambda e: e.tensor_copy(out=out.ap, in_=in_.ap), reads=[in_], writes=[out])

        def memset(out, val, eng="dve"):
            S.op(eng, lambda e: e.memset(out.ap, val), writes=[out])

        def recip(out, in_):
            S.op("dve", lambda e: e.reciprocal(out=out.ap, in_=in_.ap), reads=[in_], writes=[out])

        wada, _ = alloc("wada", 8 * 6144, BF16, at=hT_lo)
        wadav = wada.ap.rearrange("p (k c) -> p k c", k=8)
        for gb_ in hTg:
            gb_.aliases.append(wada.bufs[0])
            wada.bufs[0].aliases.append(gb_)
        brow, _ = alloc("brow", 6144, F32, at=X.bufs[0].lo)
        cactv = cact.ap.rearrange("p (k m) -> p k m", k=8)
        memset(brow[0:33, :], 0.0)
        S.dma("sp", "dcst", [
            lambda e: e.dma_start(out=cst.ap, in_=cst_d[:, :]),
            lambda e: e.dma_start(out=rows.ap, in_=rows_d[:, 0:48]),
            lambda e: e.dma_start(out=cTs.ap.rearrange("p (k b) -> p k b", k=8), in_=cT_d.rearrange("(k p) b -> p k b", p=128)),
            lambda e: e.dma_start(out=brow.ap[0:1, :], in_=bada_d[0:1, :]),
            lambda e: e.dma_start(out=brow.ap[32:33, :], in_=bada_d[0:1, :]),
        ], writes=[cst, rows, cTs, brow])
        S.dma("pool", "dada", [
            (lambda e, j=j: e.dma_start(out=wadav[:, :, j * 1024:(j + 1) * 1024],
                                        in_=wada_d[:, j * 1024:(j + 1) * 1024].rearrange("(k p) c -> p k c", p=128)))
            for j in range(6)], writes=[wada])
        S.dma("pool", "dwdt", [
            lambda e: e.dma_start(out=wdt.ap.rearrange("p (k c) -> p k c", k=8),
                                  in_=win_d[:, 2560:2576].rearrange("(k p) c -> p k c", p=128))], writes=[wdt])
        memset(ident, 0.0, "pool")
        S.op("pool", lambda e: e.affine_select(out=ident.ap, in_=ident.ap, pattern=[[-1, 128]], compare_op=ALU.not_equal,
                                               fill=1.0, base=0, channel_multiplier=1), reads=[ident], writes=[ident])
        cp(identb, ident)
        memset(triU, 1.0, "pool")
        S.op("pool", lambda e: e.affine_select(out=triU.ap, in_=triU.ap, pattern=[[1, 128]], compare_op=ALU.is_ge,
                                               fill=0.0, base=0, channel_multiplier=-1), reads=[triU], writes=[triU])
        memset(mstr, 1.0, "pool")
        S.op("pool", lambda e: e.affine_select(out=mstr.ap, in_=mstr.ap, pattern=[[-1, 128]], compare_op=ALU.is_ge,
                                               fill=0.0, base=-1, channel_multiplier=1), reads=[mstr], writes=[mstr])
        for i in range(4):
            cp(idq[32 * i:32 * i + 32, :], identb[32 * i:32 * i + 32, 32 * i:32 * i + 32])
        tt(dgc.v(dgc3), idq.v(idq.ap.unsqueeze(1).to_broadcast([128, 248, 32])),
           cst.v(cst.ap[:, C_CFW:C_CFW + 248].unsqueeze(2).to_broadcast([128, 248, 32])), ALU.mult)
        tt(dgs.v(dgs3), idq.v(idq.ap.unsqueeze(1).to_broadcast([128, 48, 32])),
           cst.v(cst.ap[:, C_CSW:C_CSW + 48].unsqueeze(2).to_broadcast([128, 48, 32])), ALU.mult)
        memset(onesf, 1.0)
        memset(onesD, 1.0 / D)
        memset(cact, 0.0)
        cTv = cTs.ap.rearrange("p (k b) -> p k b", k=8)
        for b in range(2):
            act(cact.v(cactv[:, :, 32 * b:32 * b + 1]), cTs.v(cTv[:, :, b:b + 1]), AF.Silu)
        act(arow, rows[:, R_ALOG:R_ALOG + 16], AF.Exp)
        ts(arow, arow, -1.0, None, ALU.mult)
        for j in range(12):
            pp, pa, pb_ = ppair()
            for k in range(8):
                mm(pa[0:33, :], cact.v(cactv[:, k, :]), wada.v(wadav[:, k, j * 512:(j + 1) * 512]), k == 0, k == 7)
            tt(mrow[0:33, :], pa[0:33, :], brow[0:33, j * 512:(j + 1) * 512], ALU.add)
            vec = {0: 0, 1: 0, 2: 1, 3: 1, 4: 4, 5: 4, 6: 2, 7: 2, 8: 3, 9: 3, 10: 5, 11: 5}[j]
            for b in range(2):
                for q in range(4):
                    mm(pb_[:, b * 4 + q:b * 4 + q + 1], mrow[32 * b:32 * b + 1, q * 128:(q + 1) * 128], onesf[32 * b:32 * b + 1, 0:1], True, True)
                col = b * 48 + vec * 8 + (j % 2) * 4
                cp(modv[:, col:col + 4], pb_[:, b * 4:b * 4 + 4])
        for b in range(2):
            for wch in range(2):
                sc_ = modv[:, b * 48 + (1 + 2 * wch) * 8: b * 48 + (1 + 2 * wch) * 8 + 8]
                nw_ = cst[:, (C_NMW if wch == 0 else C_NLW):(C_NMW if wch == 0 else C_NLW) + 8]
                stt(gwv[:, b * 16 + wch * 8:b * 16 + wch * 8 + 8], sc_, 1.0, nw_, ALU.add, ALU.mult)

        def wreset():
            off[0] = work_lo

        cnt = {"ring": 0, "ost": 0}

        def load_piece(src_fn):
            i = cnt["ring"] % NRING
            cnt["ring"] += 1
            slot = ring[i]
            S.dma("pool", "dr%d" % i, src_fn(slot), writes=[slot])
            return slot

        def piece_cols(w_d, c0, ncol):
            def f(slot):
                v = slot.ap[:, 0:8 * ncol].rearrange("p (k c) -> p k c", k=8)
                return [lambda e: e.dma_start(out=v, in_=w_d[:, c0:c0 + ncol].rearrange("(k p) c -> p k c", p=128))]
            return f

        def piece_rows(w_d, r0, nk):
            def f(slot):
                v = slot.ap[:, 0:nk * 1024].rearrange("p (k c) -> p k c", k=nk)
                return [lambda e: e.dma_start(out=v, in_=w_d[r0:r0 + nk * 128, :].rearrange("(k p) c -> p k c", p=128))]
            return f

        def cols_view(slot, ncol):
            return slot.ap[:, 0:8 * ncol].rearrange("p (k c) -> p k c", k=8)

        def rows_view(slot, nk):
            return slot.ap[:, 0:nk * 1024].rearrange("p (k c) -> p k c", k=nk)

        def rmsnorm_to_hT(b, wch, xn_bufs, junk):
            for t in range(8):
                act(junk, Xt[t], AF.Square, accum=ssq[:, t:t + 1])
            act(ssq[:, 8:16], ssq[:, 0:8], AF.Sqrt, bias=EPS, scale=1.0 / D)
            recip(ssq[:, 8:16], ssq[:, 8:16])
            for g in range(4):
                pairs = [ppair(), ppair()]
                for j in range(2):
                    t = 2 * g + j
                    xn = xn_bufs[t % 2]
                    ts(xn, Xt[t], ssq[:, 8 + t:9 + t], None, ALU.mult)
                    for k in range(8):
                        pp = pairs[k // 4][1 + (k % 4) // 2]
                        c0 = ((k % 4) % 2) * 256 + j * 128
                        tr(pp[:, c0:c0 + 128], xn[:, k * 128:(k + 1) * 128], ident)
                for k in range(8):
                    pp = pairs[k // 4][1 + (k % 4) // 2]
                    c0 = ((k % 4) % 2) * 256
                    act(hTs(k, g * 256, 256), pp[:, c0:c0 + 256], AF.Identity,
                        bias=modv[:, b * 48 + (2 * wch) * 8 + k: b * 48 + (2 * wch) * 8 + k + 1],
                        scale=gwv[:, b * 16 + wch * 8 + k: b * 16 + wch * 8 + k + 1])

        def resid_add(t, pp, grp, tmp):
            tt(tmp, pp, grp, ALU.mult)
            tt(Xt[t], Xt[t], tmp, ALU.add)

        for sb in range(nsb):
            b = sb // 2
            first = (sb % 2 == 0)
            tokbase = (sb % 2) * SBT
            for t in range(8):
                S.dma("sp", "dx%d" % t, [(lambda e, t=t, b=b, tokbase=tokbase: e.dma_start(out=Xt[t].ap, in_=x_d[b, tokbase + t * 128: tokbase + (t + 1) * 128, :]))],
                      writes=[Xt[t]])
            xb_p = [load_piece(piece_cols(win_d, 1024 + 512 * i, 512)) for i in range(3)]
            z_p = [load_piece(piece_cols(win_d, 512 * i, 512)) for i in range(2)]
            wos_p = [load_piece(piece_rows(wout_d, 512 * i, 4)) for i in range(2)]

            wreset()
            xn_bufs = [A("xn0", 1024), A("xn1", 1024)]
            junk = A("junk", 1024, BF16)
            rmsnorm_to_hT(b, 0, xn_bufs, junk)
            if dbg and sb == 0:
                S.dma("sp", "ddbg", [lambda e: e.dma_start(out=dbg_d["dbgH"][:, :], in_=hT.ap)], reads=[T(None, hTg)])

            wreset()
            xpre = [A("xpre%d" % i, 260, BF16) for i in range(2)]
            xbcT = [A("xbcT%d" % c, 256, BF16) for c in range(12)]
            xh2 = [A("xh%d" % i, 1024, BF16) for i in range(2)]
            Bt2 = [A("Bt%d" % i, 256, BF16) for i in range(2)]
            esb2 = [A("esb%d" % i, 48) for i in range(2)]
            xw2 = [A("xw%d" % i, 1024, BF16) for i in range(2)]
            yb2 = [A("yb%d" % i, 1024) for i in range(2)]
            dtt = A("dtt", 32)
            lat = A("lat", 32)
            acsb = A("acsb", 48)
            dtw = A("dtw", 16)
            rla = [A("rla%d" % i, 512) for i in range(2)]
            Eb = [A("E%d" % i, 512, BF16) for i in range(2)]
            MT = A("MT", 2048, BF16)
            cbm = A("cbm", 256)
            xr = A("xr", 1024, BF16)
            tF = A("tF", 1024)
            tB = A("tB", 1024)
            ycT = [A("ycT%d" % k, 128, BF16) for k in range(8)]
            ss2 = A("ss2", 4)
            dgt = tF[:, 0:128]
            if first:
                memset(Sst, 0.0)
                memset(Sb, 0.0)
                memset(halo_s, 0.0)
                for c in range(8):
                    memset(T(ucf[c][0][:, 0:30], ucf[c][1]), 0.0)
                for which in range(2):
                    for hk in range(2):
                        pp, pa, pb_ = ppair()
                        for q in range(4):
                            k = hk * 4 + q
                            col = b * 48 + (4 + which) * 8 + k
                            ts(dgt, ident, modv[:, col:col + 1], None, ALU.mult)
                            mm(pa[:, q * 128:(q + 1) * 128], onesf, dgt, True, True)
                        cp(grep[which][:, hk * 512:(hk + 1) * 512], pa[:, :])

            def X_stage(blk):
                tok0 = blk * 256
                pp, pa, pb_ = ppair()
                wdv = wdt.ap.rearrange("p (k c) -> p k c", k=8)
                for j in range(2):
                    for k in range(8):
                        mm(pa[:, j * 16:(j + 1) * 16], hTs(k, tok0 + j * 128, 128), wdt.v(wdv[:, k, :]), k == 0, k == 7)
                for j in range(2):
                    tt(dtt[:, j * 16:(j + 1) * 16], pa[:, j * 16:(j + 1) * 16], rows[:, R_DTB:R_DTB + 16], ALU.add)
                act(dtt, dtt, AF.Exp)
                act(dtt, dtt, AF.Ln, bias=1.0)
                for j in range(2):
                    tt(lat[:, j * 16:(j + 1) * 16], dtt[:, j * 16:(j + 1) * 16], arow, ALU.mult)
                yield

                for c0 in range(0, 12, 2):
                    prs = []
                    for c in (c0, c0 + 1):
                        pp, pa, pb_ = ppair()
                        prs.append((pa, pb_))
                        wv = cols_view(xb_p[c // 4], 512)
                        for k in range(8):
                            mm(pa[:, 0:256], xb_p[c // 4].v(wv[:, k, (c % 4) * 128:(c % 4 + 1) * 128]), hTs(k, tok0, 256), k == 0, k == 7)
                    for c, (pa, pb_) in zip((c0, c0 + 1), prs):
                        xp = xpre[c % 2]
                        cp(xp[:, 0:3], halo_s[:, c * 3:c * 3 + 3])
                        act(xp[:, 3:259], pa[:, 0:256], AF.Copy)
                        cp(halo_s[:, c * 3:c * 3 + 3], xp[:, 256:259])
                    for c, (pa, pb_) in zip((c0, c0 + 1), prs):
                        xp = xpre[c % 2]
                        for k in range(4):
                            for i in range(4):
                                mmq(pb_[32 * i:32 * i + 32, 0:256], dgs.v(dgs3[32 * i:32 * i + 32, c * 4 + k, :]),
                                    xp[32 * i:32 * i + 32, k:k + 256], k == 0, k == 3, i)
                        act(xbcT[c], pb_[:, 0:256], AF.Silu, bias=cst[:, C_CSB + c:C_CSB + c + 1])
                    yield
            def F_stage(g):
                j = g % 2
                g2 = g % 2
                xh, Bt, esb, xw, yb = xh2[g2], Bt2[g2], esb2[g2], xw2[g2], yb2[g2]
                dt_j = dtt[:, j * 16:(j + 1) * 16]
                la_j = lat[:, j * 16:(j + 1) * 16]
                pp, pa, pb_ = ppair()
                pab = pa.v(pa.ap.bitcast(BF16))
                pbb = pb_.v(pb_.ap.bitcast(BF16))
                for c in range(8):
                    tr(pab[:, c * 128:(c + 1) * 128], xbcT[c][:, j * 128:(j + 1) * 128], identb)
                for gg in range(2):
                    tr(pbb[:, gg * 128:(gg + 1) * 128], xbcT[8 + gg][:, j * 128:(j + 1) * 128], identb)
                cp(xh, pab[:, 0:1024])
                cp(Bt, pbb[:, 0:256])
                yield
                pq, pqa, pqb = ppair()
                mm(pqa[:, 0:16], triU, la_j, True, True)
                mm(pqa[:, 16:32], onesf, la_j, True, True)
                for gg in range(2):
                    mm(pqb[:, gg * 128:(gg + 1) * 128], xbcT[8 + gg][:, j * 128:(j + 1) * 128], xbcT[10 + gg][:, j * 128:(j + 1) * 128], True, True)
                act(acsb[:, 0:32], pqa[:, 0:32], AF.Copy)
                tt(acsb[:, 32:48], acsb[:, 16:32], acsb[:, 0:16], ALU.subtract)
                act(esb, acsb, AF.Exp)
                tt(dtw, dt_j, esb[:, 32:48], ALU.mult)
                cbv = cbm.ap.rearrange("p (g l) -> p g l", g=2)
                tt(cbm.v(cbv), pqb.v(pqb.ap[:, 0:256].rearrange("p (g l) -> p g l", g=2)),
                   triU.v(triU.ap.unsqueeze(1).to_broadcast([128, 2, 128])), ALU.mult)
                xh3 = xh.v(xh.ap.rearrange("p (h q) -> p h q", h=16))
                tt(xr.v(xr.ap.rearrange("p (h q) -> p h q", h=16)), xh3, dt_j.v(dt_j.ap.unsqueeze(2).to_broadcast([128, 16, 64])), ALU.mult)
                tt(xw.v(xw.ap.rearrange("p (h q) -> p h q", h=16)), xh3, dtw.v(dtw.ap.unsqueeze(2).to_broadcast([128, 16, 64])), ALU.mult)
                yield
                MTv = MT.ap.rearrange("p (h l) -> p h l", h=16)
                def mk_rla(q4):
                    rl = rla[q4 % 2]
                    rl3 = rl.v(rl.ap.rearrange("p (h l) -> p h l", h=4))
                    la4 = la_j[:, q4 * 4:(q4 + 1) * 4]
                    tt(rl3, triU.v(triU.ap.unsqueeze(1).to_broadcast([128, 4, 128])),
                       la4.v(la4.ap.unsqueeze(2).to_broadcast([128, 4, 128])), ALU.mult, eng="pool")

                mk_rla(0)
                mk_rla(1)
                for q4 in range(4):
                    rl = rla[q4 % 2]
                    Eq = Eb[q4 % 2]
                    pg, pga, pgb = ppair()
                    mm(pga[:, :], mstr, rl, True, True)
                    if q4 < 2:
                        mk_rla(q4 + 2)
                    act(Eq, pga, AF.Exp)
                    gg = q4 // 2
                    tt(MT.v(MTv[:, q4 * 4:(q4 + 1) * 4, :]), Eq.v(Eq.ap.rearrange("p (h l) -> p h l", h=4)),
                       cbm.v(cbv[:, gg:gg + 1, :].to_broadcast([128, 4, 128])), ALU.mult)
                    yield
                tt(tF.v(tF.ap.rearrange("p (h q) -> p h q", h=16)), xh3,
                   rows.v(rows.ap[:, R_DSK:R_DSK + 16].unsqueeze(2).to_broadcast([128, 16, 64])), ALU.mult)
                pd, pda, pdb = ppair()
                xrv = xr.ap.rearrange("p (h q) -> p h q", h=16)
                for h in range(16):
                    pdd = pda if h < 8 else pdb
                    mm(pdd[:, (h % 8) * 64:(h % 8 + 1) * 64], MT.v(MTv[:, h, :]), xr.v(xrv[:, h, :]), True, True)
                tt(yb, pd, tF, ALU.add)
                yield

            def S_stage(g):
                j = g % 2
                g2 = g % 2
                Bt, esb, xw, yb = Bt2[g2], esb2[g2], xw2[g2], yb2[g2]
                po, poa, pob = ppair()
                Sbv = Sb.ap.rearrange("p (g m) -> p g m", g=2)
                for gg, pgg in enumerate((poa, pob)):
                    mm(pgg[:, :], xbcT[10 + gg][:, j * 128:(j + 1) * 128], Sb.v(Sbv[:, gg, :]), True, True)
                tt(tB.v(tB.ap.rearrange("p (h q) -> p h q", h=16)), po.v(po.ap.rearrange("p (h q) -> p h q", h=16)),
                   esb.v(esb.ap[:, 0:16].unsqueeze(2).to_broadcast([128, 16, 64])), ALU.mult)
                tt(yb, yb, tB, ALU.add)
                pst, psa, psb = ppair()
                Btv = Bt.ap.rearrange("p (g n) -> p g n", g=2)
                xwv = xw.ap.rearrange("p (g m) -> p g m", g=2)
                for gg, pgg in enumerate((psa, psb)):
                    mm(pgg[:, :], Bt.v(Btv[:, gg, :]), xw.v(xwv[:, gg, :]), True, True)
                S3 = Sst.v(Sst.ap.rearrange("p (h q) -> p h q", h=16))
                tt(S3, S3, esb.v(esb.ap[:, 16:32].unsqueeze(2).to_broadcast([128, 16, 64])), ALU.mult)
                tt(Sst, Sst, pst, ALU.add)
                act(Sb, Sst, AF.Copy)

            def B_stage(g):
                t = g
                tk = g * 128
                yb = yb2[g % 2]
                zs = tB
                gnb = tB
                pz, pza, pzb = ppair()
                for half, pzz in enumerate((pza, pzb)):
                    zv = cols_view(z_p[half], 512)
                    for k in range(8):
                        mm(pzz[:, :], hTs(k, tk, 128), z_p[half].v(zv[:, k, :]), k == 0, k == 7)
                act(zs, pz, AF.Silu)
                tt(yb, yb, zs, ALU.mult)
                yield
                pj, pja, pjb = ppair()
                for gg, pjj in enumerate((pja, pjb)):
                    act(pjj[:, :], yb[:, gg * 512:(gg + 1) * 512], AF.Square, accum=ss2[:, gg:gg + 1])
                act(ss2[:, 2:4], ss2[:, 0:2], AF.Ln, bias=EPS, scale=1.0 / 512)
                act(ss2[:, 2:4], ss2[:, 2:4], AF.Exp, scale=-0.5)
                for gg in range(2):
                    ts(gnb[:, gg * 512:(gg + 1) * 512], yb[:, gg * 512:(gg + 1) * 512], ss2[:, 2 + gg:3 + gg], None, ALU.mult)
                yield
                pt, pta, ptb = ppair()
                for k in range(8):
                    tr(pt[:, k * 128:(k + 1) * 128], gnb[:, k * 128:(k + 1) * 128], ident)
                for k in range(8):
                    act(ycT[k], pt[:, k * 128:(k + 1) * 128], AF.Copy, scale=cst[:, C_SNW + k:C_SNW + k + 1])
                yield
                px, pxa, pxb = ppair()
                for half, pxx in enumerate((pxa, pxb)):
                    for k in range(8):
                        wv = rows_view(wos_p[k // 4], 4)
                        mm(pxx[:, :], ycT[k], wos_p[k // 4].v(wv[:, k % 4, half * 512:(half + 1) * 512]), k == 0, k == 7)
                resid_add(t, px, grep[0], yb)
                yield

            def chain(*gens):
                for gn in gens:
                    yield from gn

            def merge(ga, na, gb, nb):
                ia = ib = 0
                da = db = False
                while not (da and db):
                    ta = (ia + 1) * nb if not da else None
                    tb_ = (ib + 1) * na if not db else None
                    pick_a = (not da) and (db or ta <= tb_)
                    if pick_a:
                        try:
                            next(ga)
                            ia += 1
                        except StopIteration:
                            da = True
                    else:
                        try:
                            next(gb)
                            ib += 1
                        except StopIteration:
                            db = True

            for _ in X_stage(0):
                pass
            for _ in F_stage(0):
                pass
            def SB_stage(g):
                S_stage(g)
                yield
                yield from B_stage(g)

            p3_pieces = None
            for g in range(8):
                if g == 6:
                    gb0 = load_piece(piece_cols(win_d, 3600, 512))
                    ga0 = load_piece(piece_cols(win_d, 2576, 512))
                    gb1 = load_piece(piece_cols(win_d, 3600 + 512, 512))
                    ga1 = load_piece(piece_cols(win_d, 2576 + 512, 512))
                    p3_pieces = ([ga0, ga1], [gb0, gb1])
                front = []
                nf = 0
                if g % 2 == 1 and g < 7:
                    front.append(X_stage((g + 1) // 2))
                    nf += 7
                if g < 7:
                    front.append(F_stage(g + 1))
                    nf += 8
                if front:
                    merge(SB_stage(g), 5, chain(*front), nf)
                else:
                    for _ in SB_stage(g):
                        pass
            if dbg and sb == 0:
                S.dma("sp", "ddbg", [lambda e: e.dma_start(out=dbg_d["dbgA"].rearrange("(t p) d -> p t d", p=128), in_=Xv)], reads=[X])

            ga_p, gb_p = p3_pieces
            woc_p = [load_piece(piece_rows(wout_d, 1024 + 512 * i, 4)) for i in range(2)]
            wreset()
            sig = [A("sig%d" % i, TB3) for i in range(2)]
            vv = [A("v%d" % c, TB3) for c in range(8)]
            vsq = [A("vsq%d" % i, TB3) for i in range(2)]
            mean_sb = A("mean_sb", TB3)
            rstd_sb = A("rstd_sb", TB3)
            tln = [A("tln%d" % i, TB3) for i in range(2)]
            ucT = [A("ucT%d" % c, TB3, BF16) for c in range(8)]
            tmpx = A("tmpx3", 1024)
            for blk in range(SBT // TB3):
                tok0 = blk * TB3
                psta, pstb = plong
                nrot[0] = 3

                def glu(c):
                    pp, pa, pb_ = ppair()
                    gbv = cols_view(gb_p[c // 4], 512)
                    gav = cols_view(ga_p[c // 4], 512)
                    for k in range(8):
                        mm(pa[:, 0:TB3], gb_p[c // 4].v(gbv[:, k, (c % 4) * 128:(c % 4 + 1) * 128]), hTs(k, tok0, TB3), k == 0, k == 7)
                    for k in range(8):
                        mm(pb_[:, 0:TB3], ga_p[c // 4].v(gav[:, k, (c % 4) * 128:(c % 4 + 1) * 128]), hTs(k, tok0, TB3), k == 0, k == 7)
                    return pa, pb_

                def stats(c):
                    mm(psta[:, 0:TB3], onesD, vv[c], c == 0, c == 7)
                    mm(pstb[:, 0:TB3], onesD, vsq[c % 2], c == 0, c == 7)

                nxt = glu(0)
                for c in range(8):
                    pa, pb_ = nxt
                    sg = sig[c % 2]
                    act(sg, pa[:, 0:TB3], AF.Sigmoid)
                    uap, uh, ub = ucf[c]
                    tt(T(uap[:, 30:30 + TB3], ub), pb_[:, 0:TB3], sg, ALU.mult)
                    if c < 7:
                        nxt = glu(c + 1)
                    if c > 0:
                        stats(c - 1)
                    v = vv[c]
                    pp2, pc, pc2 = ppair()
                    for k in range(31):
                        for i in range(4):
                            mmq(pc[32 * i:32 * i + 32, 0:TB3], dgc.v(dgc3[32 * i:32 * i + 32, c * 31 + k, :]),
                                T(uap[32 * i:32 * i + 32, k:k + TB3], [uh, ub]), k == 0, k == 30, i)
                    act(v, pc[:, 0:TB3], AF.Identity, bias=cst[:, C_CFB + c:C_CFB + c + 1])
                    act(vsq[c % 2], v, AF.Square)
                stats(7)
                for c in range(8):
                    uap, uh, ub = ucf[c]
                    cp(T(uap[:, 0:30], uh), T(uap[:, TB3:TB3 + 30], ub))
                act(mean_sb, psta[:, 0:TB3], AF.Copy)
                tt(rstd_sb, mean_sb, mean_sb, ALU.mult)
                tt(rstd_sb, pstb[:, 0:TB3], rstd_sb, ALU.subtract)
                act(rstd_sb, rstd_sb, AF.Sqrt, bias=EPS)
                recip(rstd_sb, rstd_sb)
                for c in range(8):
                    tl = tln[c % 2]
                    tt(tl, vv[c], mean_sb, ALU.subtract)
                    tt(tl, tl, rstd_sb, ALU.mult)
                    act(ucT[c], tl, AF.Silu, bias=cst[:, C_LNB + c:C_LNB + c + 1], scale=cst[:, C_LNW + c:C_LNW + c + 1])
                for j in range(TB3 // 128):
                    t = blk * (TB3 // 128) + j
                    px, pxa, pxb = ppair()
                    for half, pxx in enumerate((pxa, pxb)):
                        for k in range(8):
                            wv = rows_view(woc_p[k // 4], 4)
                            mm(pxx[:, :], ucT[k][:, j * 128:(j + 1) * 128], woc_p[k // 4].v(wv[:, k % 4, half * 512:(half + 1) * 512]), k == 0, k == 7)
                    resid_add(t, px, grep[0], tmpx)
            if dbg and sb == 0:
                S.dma("sp", "ddbg", [lambda e: e.dma_start(out=dbg_d["dbgB"].rearrange("(t p) d -> p t d", p=128), in_=Xv)], reads=[X])

            nrot[0] = 3
            wreset()
            xn_bufs = [A("xn0b", 1024), A("xn1b", 1024)]
            junk = A("junkb", 1024, BF16)
            rr = [A("rr%d" % i, 512) for i in range(2)]
            hid = [A("hid%d" % f, 512, BF16) for f in range(8)]
            tmpx = A("tmpx4", 1024)
            rmsnorm_to_hT(b, 1, xn_bufs, junk)
            for q4 in range(4):
                w1p = [load_piece(piece_cols(w1_d, 1024 * q4 + 512 * i, 512)) for i in range(2)]
                w2p = [load_piece(piece_rows(w2_d, 1024 * q4 + 512 * i, 4)) for i in range(2)]
                for blk in range(2):
                    for f in range(8):
                        pp, pa, pb_ = ppair()
                        w1v = cols_view(w1p[f // 4], 512)
                        for k in range(8):
                            mm(pa[:, :], w1p[f // 4].v(w1v[:, k, (f % 4) * 128:(f % 4 + 1) * 128]), hTs(k, blk * 512, 512), k == 0, k == 7)
                        r = rr[f % 2]
                        act(r, pa, AF.Relu)
                        tt(hid[f], r, r, ALU.mult)
                    for j in range(4):
                        t = blk * 4 + j
                        px, pxa, pxb = ppair()
                        for half, pxx in enumerate((pxa, pxb)):
                            for f in range(8):
                                w2v = rows_view(w2p[f // 4], 4)
                                mm(pxx[:, :], hid[f][:, j * 128:(j + 1) * 128], w2p[f // 4].v(w2v[:, f % 4, half * 512:(half + 1) * 512]), f == 0, f == 7)
                        resid_add(t, px, grep[1], tmpx)
            if dbg and sb == 0:
                S.dma("sp", "ddbg", [lambda e: e.dma_start(out=dbg_d["dbgC"].rearrange("(t p) d -> p t d", p=128), in_=Xv)], reads=[X])

            wreset()
            junk = A("junkf", 1024, BF16)
            ost = [A("ost%d" % i, 1024) for i in range(2)]
            wfin = A("wfin", 1024)
            S.dma("sp", "dwf", [lambda e: e.dma_start(out=wfin.ap, in_=rows_d[:, R_WFIN:R_WFIN + 1024])], writes=[wfin])
            for t in range(8):
                act(junk, Xt[t], AF.Square, accum=ssq[:, t:t + 1])
            act(ssq[:, 8:16], ssq[:, 0:8], AF.Sqrt, bias=EPS, scale=1.0 / D)
            recip(ssq[:, 8:16], ssq[:, 8:16])
            for t in range(8):
                i = cnt["ost"] % 2
                cnt["ost"] += 1
                stt(ost[i], Xt[t], ssq[:, 8 + t:9 + t], wfin, ALU.mult, ALU.mult)
                S.dma("sp", "dout%d" % i, [(lambda e, t=t, i=i, b=b, tokbase=tokbase, o=ost[i]: e.dma_start(out=out_d[b, tokbase + t * 128: tokbase + (t + 1) * 128, :], in_=o.ap))],
                      reads=[ost[i]])

        fin = S.op("sp", lambda e: e.nop())
        for nm in ("dma:dout0", "dma:dout1", "dma:ddbg"):
            if nm in S.last_dma:
                fin.deps.append(S.last_dma[nm])
        S.plan()
        S.emit(block, sems)
    return nc


_NC_CACHE = {}


def kernel(**inputs):
    inp = {k: np.asarray(v) for k, v in inputs.items()}
    if "nc" not in _NC_CACHE:
        _NC_CACHE["nc"] = build_nc()
    nc = _NC_CACHE["nc"]
    cst, rows = pack_consts(inp)
    shared = {
        "w_ada": np.ascontiguousarray(inp["w_ada"][0]), "b_ada": np.ascontiguousarray(inp["b_ada"][0:1]),
        "w_in": np.ascontiguousarray(inp["w_in"][0]), "w_out": np.ascontiguousarray(inp["w_out"][0]),
        "w_mlp1": np.ascontiguousarray(inp["w_mlp1"][0]), "w_mlp2": np.ascontiguousarray(inp["w_mlp2"][0]),
        "cst": cst, "rows": rows,
    }
    in_maps = []
    for i in range(NCORES):
        m = dict(shared)
        m["x"] = np.ascontiguousarray(inp["x"][2 * i:2 * i + 2])
        m["cT"] = np.ascontiguousarray(inp["c"][2 * i:2 * i + 2].T)
        in_maps.append(m)
    res = run_bass_kernel_spmd(nc, in_maps, core_ids=list(range(NCORES)))
    return np.concatenate([r["out"] for r in res.results], axis=0).astype(np.float32)
```

```python
import numpy as np
import concourse.bass as bass
import concourse.mybir as mybir
from concourse.bass_utils import run_bass_kernel_spmd

F32 = mybir.dt.float32
BF16 = mybir.dt.bfloat16
AF = mybir.ActivationFunctionType
ALU = mybir.AluOpType

D = 1024
SEQ = 2048
NB = 16
NCORES = 8
D_IN = 4624
EPS = 1e-5
SBT = 1024
NRING = 8
TB3 = 512


class Buf:
    __slots__ = ("name", "w", "rs", "aliases", "space", "lo", "hi")

    def __init__(self, reg, name, space=None, lo=0, hi=0):
        self.name = name
        self.w = None
        self.rs = []
        self.aliases = []
        self.space = space
        self.lo = lo
        self.hi = hi
        if space is not None:
            for o in reg:
                if o.space == space and o.lo < hi and lo < o.hi:
                    o.aliases.append(self)
                    self.aliases.append(o)
            reg.append(self)


class T:
    __slots__ = ("ap", "bufs")

    def __init__(self, ap, bufs):
        self.ap = ap
        self.bufs = bufs if isinstance(bufs, (list, tuple)) else [bufs]

    def __getitem__(self, key):
        return T(self.ap[key], self.bufs)

    def v(self, ap):
        return T(ap, self.bufs)


class Op:
    __slots__ = ("stream", "issue", "seq", "fn", "deps", "sig", "semval", "waits", "clock", "is_dma", "nd")

    def __init__(self):
        self.sig = False
        self.semval = 0
        self.waits = []
        self.clock = None


COMPUTE = ("pe", "act", "dve", "pool", "sp")


class Sched:
    def __init__(self):
        self.ops = []
        self.count = {}
        self.by_stream = {}
        self.last_dma = {}

    def _deps(self, reads, writes):
        deps = []
        for t in reads:
            for b in t.bufs:
                if b.w is not None:
                    deps.append(b.w)
                for a in b.aliases:
                    if a.w is not None:
                        deps.append(a.w)
        for t in writes:
            for b in t.bufs:
                for bb in [b] + b.aliases:
                    if bb.w is not None:
                        deps.append(bb.w)
                    deps.extend(bb.rs)
        return deps

    def _commit(self, op, reads, writes):
        for t in writes:
            for b in t.bufs:
                b.w = op
                b.rs = []
        for t in reads:
            for b in t.bufs:
                b.rs.append(op)

    def op(self, eng, fn, reads=(), writes=()):
        o = Op()
        o.stream = eng
        o.issue = eng
        o.is_dma = False
        o.fn = fn
        o.deps = self._deps(reads, writes)
        self.count[eng] = self.count.get(eng, 0) + 1
        o.seq = self.count[eng]
        self._commit(o, reads, writes)
        self.ops.append(o)
        self.by_stream.setdefault(eng, []).append(o)
        return o

    def dma(self, queue, sem_name, fns, reads=(), writes=()):
        o = Op()
        o.stream = "dma:" + sem_name
        o.issue = queue
        o.is_dma = True
        o.fn = fns
        o.nd = len(fns)
        o.deps = self._deps(reads, writes)
        prev = self.last_dma.get(o.stream)
        if prev is not None:
            o.deps.append(prev)
        self.last_dma[o.stream] = o
        self.count[o.stream] = self.count.get(o.stream, 0) + 1
        o.seq = self.count[o.stream]
        self._commit(o, reads, writes)
        self.ops.append(o)
        self.by_stream.setdefault(o.stream, []).append(o)
        return o

    def plan(self):
        known = {e: {} for e in COMPUTE}
        for o in self.ops:
            k = known[o.issue]
            need = {}
            for d in o.deps:
                if d.stream == "pe" and o.stream == "pe":
                    continue
                if k.get(d.stream, 0) >= d.seq:
                    continue
                if need.get(d.stream, 0) < d.seq:
                    need[d.stream] = d.seq
            for s, q in need.items():
                if k.get(s, 0) >= q:
                    continue
                d = self.by_stream[s][q - 1]
                d.sig = True
                o.waits.append(d)
                for s2, q2 in d.clock.items():
                    if k.get(s2, 0) < q2:
                        k[s2] = q2
            o.clock = dict(k)
            o.clock[o.stream] = o.seq
        for s, lst in self.by_stream.items():
            c = 0
            for o in lst:
                if o.is_dma:
                    c += 16 * o.nd
                    o.semval = c
                elif o.sig:
                    c += 1
                    o.semval = c

    def emit(self, block, sems):
        per = {e: [o for o in self.ops if o.issue == e] for e in COMPUTE}

        def run(eng, lst):
            for o in lst:
                ws = [(sems[d.stream], d.semval) for d in o.waits]
                if o.is_dma:
                    for (s, v) in ws:
                        eng.wait_ge(s, v)
                    for f in o.fn:
                        f(eng).then_inc(sems[o.stream], 16)
                else:
                    for (s, v) in ws[1:]:
                        eng.wait_ge(s, v)
                    ins = o.fn(eng)
                    if ws:
                        ins._wait_ge(ws[0][0], ws[0][1])
                    if o.sig:
                        ins.then_inc(sems[o.stream], 1)

        @block.tensor
        def _(e):
            run(e, per["pe"])

        @block.scalar
        def _(e):
            run(e, per["act"])

        @block.vector
        def _(e):
            run(e, per["dve"])

        @block.gpsimd
        def _(e):
            run(e, per["pool"])

        @block.sync
        def _(e):
            run(e, per["sp"])


C_NMW, C_NLW, C_SNW, C_LNW, C_LNB, C_CFB, C_CSB, C_CSW, C_CFW = 0, 8, 16, 24, 32, 40, 48, 60, 108
NCST = 108 + 8 * 31
R_DTB, R_ALOG, R_DSK, R_WFIN = 0, 16, 32, 48
NROW = 48 + 1024


def _fm(v):
    return np.ascontiguousarray(v.reshape(-1, 128).T)


def pack_consts(inp):
    cst = np.zeros((128, NCST), np.float32)
    cst[:, C_NMW:C_NMW + 8] = _fm(inp["norm_mix_w"][0])
    cst[:, C_NLW:C_NLW + 8] = _fm(inp["norm_mlp_w"][0])
    cst[:, C_SNW:C_SNW + 8] = _fm(inp["ssd_norm_w"][0])
    cst[:, C_LNW:C_LNW + 8] = _fm(inp["cf_ln_w"][0])
    cst[:, C_LNB:C_LNB + 8] = _fm(inp["cf_ln_b"][0])
    cst[:, C_CFB:C_CFB + 8] = _fm(inp["conv_cf_b"][0])
    cst[:, C_CSB:C_CSB + 12] = _fm(inp["conv_ssd_b"][0])
    w = inp["conv_ssd_w"][0]
    cst[:, C_CSW:C_CSW + 48] = w.reshape(4, 12, 128).transpose(2, 1, 0).reshape(128, 48)
    w = inp["conv_cf_w"][0]
    cst[:, C_CFW:C_CFW + 248] = w.reshape(31, 8, 128).transpose(2, 1, 0).reshape(128, 248)
    rows = np.zeros((128, NROW), np.float32)
    rows[:, R_DTB:R_DTB + 16] = inp["dt_bias"][0][None, :]
    rows[:, R_ALOG:R_ALOG + 16] = inp["a_log"][0][None, :]
    rows[:, R_DSK:R_DSK + 16] = inp["d_skip"][0][None, :]
    rows[:, R_WFIN:R_WFIN + 1024] = inp["norm_final_w"][None, :]
    return cst, rows


def build_nc(nsb=4, dbg=False):
    nc = bass.Bass("TRN2", target_bir_lowering=False)
    nseq = (nsb + 1) // 2
    x_d = nc.dram_tensor("x", [2, SEQ, D], F32, kind="ExternalInput").ap()
    cT_d = nc.dram_tensor("cT", [D, 2], F32, kind="ExternalInput").ap()
    wada_d = nc.dram_tensor("w_ada", [D, 6 * D], F32, kind="ExternalInput").ap()
    bada_d = nc.dram_tensor("b_ada", [1, 6 * D], F32, kind="ExternalInput").ap()
    win_d = nc.dram_tensor("w_in", [D, D_IN], F32, kind="ExternalInput").ap()
    wout_d = nc.dram_tensor("w_out", [2 * D, D], F32, kind="ExternalInput").ap()
    w1_d = nc.dram_tensor("w_mlp1", [D, 4 * D], F32, kind="ExternalInput").ap()
    w2_d = nc.dram_tensor("w_mlp2", [4 * D, D], F32, kind="ExternalInput").ap()
    cst_d = nc.dram_tensor("cst", [128, NCST], F32, kind="ExternalInput").ap()
    rows_d = nc.dram_tensor("rows", [128, NROW], F32, kind="ExternalInput").ap()
    out_d = nc.dram_tensor("out", [2, SEQ, D], F32, kind="ExternalOutput").ap()
    dbg_d = {}
    if dbg:
        for nm in ("dbgA", "dbgB", "dbgC"):
            dbg_d[nm] = nc.dram_tensor(nm, [SBT, D], F32, kind="ExternalOutput").ap()
        dbg_d["dbgH"] = nc.dram_tensor("dbgH", [128, 8 * SBT], BF16, kind="ExternalOutput").ap()

    S = Sched()
    reg = []
    NW = 53200
    sem_names = ["pe", "act", "dve", "pool", "dcst", "dada", "dout0", "dout1", "ddbg", "dwf", "dwdt"] + \
        ["dx%d" % i for i in range(8)] + ["dr%d" % i for i in range(NRING)]
    import contextlib
    with contextlib.ExitStack() as es:
        arena = es.enter_context(nc.sbuf_tensor("arena", [128, NW], F32))
        ps = es.enter_context(nc.psum_tensor("ps", [128, 4096], F32))
        sems = {}
        for nm in sem_names:
            h = es.enter_context(nc.semaphore("s_" + nm))
            sems[nm if nm in COMPUTE else "dma:" + nm] = h
        block = es.enter_context(nc.Block())

        off = [0]

        def alloc(name, n, dt=F32, at=None):
            nw = n if dt == F32 else (n + 1) // 2
            if at is None:
                lo = off[0]
                off[0] += nw
                assert off[0] <= NW, (name, off[0])
            else:
                lo = at
            ap = arena[:, lo:lo + nw]
            if dt != F32:
                ap = ap.bitcast(dt)
            return T(ap, Buf(reg, name, "sb", lo, lo + nw)), lo

        def A(name, n, dt=F32):
            return alloc(name, n, dt)[0]

        pbank = [T(ps[:, 512 * i:512 * (i + 1)], Buf(reg, "pb%d" % i)) for i in range(8)]
        prr = [0]
        nrot = [3]

        def ppair():
            i = prr[0] % nrot[0]
            prr[0] = (i + 1) % nrot[0]
            t = T(ps[:, 1024 * i:1024 * (i + 1)], [pbank[2 * i].bufs[0], pbank[2 * i + 1].bufs[0]])
            return t, pbank[2 * i], pbank[2 * i + 1]

        plong = (pbank[6], pbank[7])

        X = A("X", 8 * 1024)
        Xv = X.ap.rearrange("p (t d) -> p t d", t=8)
        Xt = []
        for t in range(8):
            lo = X.bufs[0].lo + t * 1024
            Xt.append(T(Xv[:, t, :], Buf(reg, "X%d" % t, "sb", lo, lo + 1024)))
        grep = [A("grep%d" % w, 1024) for w in range(2)]
        cst = A("cst", NCST)
        rows = A("rows", 48)
        modv = A("modv", 96)
        gwv = A("gwv", 32)
        ident = A("ident", 128)
        identb = A("identb", 128, BF16)
        triU = A("triU", 128)
        mstr = A("mstr", 128)
        onesf = A("onesf", 128)
        onesD = A("onesD", 128)
        arow = A("arow", 16)
        wdt = A("wdt", 8 * 16, BF16)
        Sst = A("Sst", 1024)
        Sb = A("Sb", 1024, BF16)
        halo_s = A("halo_s", 36, BF16)
        ucf = []
        for c in range(8):
            ut_, lo_ = alloc("ucf%d" % c, 30 + TB3, BF16)
            ucf.append((ut_.ap, Buf(reg, "ucfh%d" % c), Buf(reg, "ucfb%d" % c)))
        ssq = A("ssq", 16)
        idq = A("idq", 32, BF16)
        dgc = A("dgc", 248 * 32, BF16)
        dgs = A("dgs", 48 * 32, BF16)
        dgc3 = dgc.ap.rearrange("p (m j) -> p m j", j=32)
        dgs3 = dgs.ap.rearrange("p (m j) -> p m j", j=32)
        hT, hT_lo = alloc("hT", 8 * SBT, BF16)
        hTv = hT.ap.rearrange("p (k t) -> p k t", k=8)
        hTg = [Buf(reg, "hTg%d" % g) for g in range(4)]

        def hTs(k, t0, n):
            return T(hTv[:, k, t0:t0 + n], [hTg[g] for g in range(t0 // 256, (t0 + n - 1) // 256 + 1)])
        ring = []
        for i in range(NRING):
            ring.append(A("ring%d" % i, 4096, BF16))
        work_lo = off[0]
        assert hT_lo + 8 * 6144 // 2 <= NW
        off[0] = hT_lo + 8 * 6144 // 2
        cTs = A("cTs", 16)
        cact = A("cact", 8 * 33, BF16)
        mrow = A("mrow", 512)
        off[0] = work_lo

        def mm(out, lhsT, rhs, start, stop):
            S.op("pe", lambda e: e.matmul(out.ap, lhsT.ap, rhs.ap, start=start, stop=stop), reads=[lhsT, rhs], writes=[out])

        def mmq(out, lhsT, rhs, start, stop, i):
            S.op("pe", lambda e: e.matmul(out.ap, lhsT.ap, rhs.ap, start=start, stop=stop, tile_position=(32 * i, 32 * i)),
                 reads=[lhsT, rhs], writes=[out])

        def tr(out, in_, idn):
            S.op("pe", lambda e: e.transpose(out=out.ap, in_=in_.ap, identity=idn.ap), reads=[in_, idn], writes=[out])

        def act(out, in_, func, bias=None, scale=None, accum=None, extra_r=(), extra_w=()):
            kw = {}
            rd = [in_] + list(extra_r)
            wr = [out] + list(extra_w)
            if bias is not None:
                if isinstance(bias, T):
                    kw["bias"] = bias.ap
                    rd.append(bias)
                else:
                    kw["bias"] = bias
            if scale is not None:
                if isinstance(scale, T):
                    kw["scale"] = scale.ap
                    rd.append(scale)
                else:
                    kw["scale"] = scale
            if accum is not None:
                kw["accum_out"] = accum.ap
                wr.append(accum)
            S.op("act", lambda e: e.activation(out=out.ap, in_=in_.ap, func=func, **kw), reads=rd, writes=wr)

        def tt(out, in0, in1, op, eng="dve"):
            S.op(eng, lambda e: e.tensor_tensor(out=out.ap, in0=in0.ap, in1=in1.ap, op=op), reads=[in0, in1], writes=[out])

        def ts(out, in0, s1, s2, op0, op1=None, eng="dve"):
            rd = [in0]
            a1 = s1.ap if isinstance(s1, T) else s1
            a2 = s2.ap if isinstance(s2, T) else s2
            if isinstance(s1, T):
                rd.append(s1)
            if isinstance(s2, T):
                rd.append(s2)
            if op1 is None:
                assert op0 == ALU.mult
                S.op(eng, lambda e: e.tensor_scalar_mul(out=out.ap, in0=in0.ap, scalar1=a1), reads=rd, writes=[out])
            else:
                S.op(eng, lambda e: e.tensor_scalar(out=out.ap, in0=in0.ap, scalar1=a1, scalar2=a2, op0=op0, op1=op1), reads=rd, writes=[out])

        def stt(out, in0, sc, in1, op0, op1, eng="dve"):
            rd = [in0, in1]
            a = sc.ap if isinstance(sc, T) else sc
            if isinstance(sc, T):
                rd.append(sc)
            S.op(eng, lambda e: e.scalar_tensor_tensor(out=out.ap, in0=in0.ap, scalar=a, in1=in1.ap, op0=op0, op1=op1), reads=rd, writes=[out])

        def cp(out, in_, eng="dve"):
            S.op(eng, lambda e: e.tensor_copy(out=out.ap, in_=in_.ap), reads=[in_], writes=[out])

        def memset(out, val, eng="dve"):
            S.op(eng, lambda e: e.memset(out.ap, val), writes=[out])

        def recip(out, in_):
            S.op("dve", lambda e: e.reciprocal(out=out.ap, in_=in_.ap), reads=[in_], writes=[out])

        wada, _ = alloc("wada", 8 * 6144, BF16, at=hT_lo)
        wadav = wada.ap.rearrange("p (k c) -> p k c", k=8)
        for gb_ in hTg:
            gb_.aliases.append(wada.bufs[0])
            wada.bufs[0].aliases.append(gb_)
        brow, _ = alloc("brow", 6144, F32, at=X.bufs[0].lo)
        cactv = cact.ap.rearrange("p (k m) -> p k m", k=8)
        memset(brow[0:33, :], 0.0)
        S.dma("sp", "dcst", [
            lambda e: e.dma_start(out=cst.ap, in_=cst_d[:, :]),
            lambda e: e.dma_start(out=rows.ap, in_=rows_d[:, 0:48]),
            lambda e: e.dma_start(out=cTs.ap.rearrange("p (k b) -> p k b", k=8), in_=cT_d.rearrange("(k p) b -> p k b", p=128)),
            lambda e: e.dma_start(out=brow.ap[0:1, :], in_=bada_d[0:1, :]),
            lambda e: e.dma_start(out=brow.ap[32:33, :], in_=bada_d[0:1, :]),
        ], writes=[cst, rows, cTs, brow])
        S.dma("pool", "dada", [
            (lambda e, j=j: e.dma_start(out=wadav[:, :, j * 1024:(j + 1) * 1024],
                                        in_=wada_d[:, j * 1024:(j + 1) * 1024].rearrange("(k p) c -> p k c", p=128)))
            for j in range(6)], writes=[wada])
        S.dma("pool", "dwdt", [
            lambda e: e.dma_start(out=wdt.ap.rearrange("p (k c) -> p k c", k=8),
                                  in_=win_d[:, 2560:2576].rearrange("(k p) c -> p k c", p=128))], writes=[wdt])
        memset(ident, 0.0, "pool")
        S.op("pool", lambda e: e.affine_select(out=ident.ap, in_=ident.ap, pattern=[[-1, 128]], compare_op=ALU.not_equal,
                                               fill=1.0, base=0, channel_multiplier=1), reads=[ident], writes=[ident])
        cp(identb, ident)
        memset(triU, 1.0, "pool")
        S.op("pool", lambda e: e.affine_select(out=triU.ap, in_=triU.ap, pattern=[[1, 128]], compare_op=ALU.is_ge,
                                               fill=0.0, base=0, channel_multiplier=-1), reads=[triU], writes=[triU])
        memset(mstr, 1.0, "pool")
        S.op("pool", lambda e: e.affine_select(out=mstr.ap, in_=mstr.ap, pattern=[[-1, 128]], compare_op=ALU.is_ge,
                                               fill=0.0, base=-1, channel_multiplier=1), reads=[mstr], writes=[mstr])
        for i in range(4):
            cp(idq[32 * i:32 * i + 32, :], identb[32 * i:32 * i + 32, 32 * i:32 * i + 32])
        tt(dgc.v(dgc3), idq.v(idq.ap.unsqueeze(1).to_broadcast([128, 248, 32])),
           cst.v(cst.ap[:, C_CFW:C_CFW + 248].unsqueeze(2).to_broadcast([128, 248, 32])), ALU.mult)
        tt(dgs.v(dgs3), idq.v(idq.ap.unsqueeze(1).to_broadcast([128, 48, 32])),
           cst.v(cst.ap[:, C_CSW:C_CSW + 48].unsqueeze(2).to_broadcast([128, 48, 32])), ALU.mult)
        memset(onesf, 1.0)
        memset(onesD, 1.0 / D)
        memset(cact, 0.0)
        cTv = cTs.ap.rearrange("p (k b) -> p k b", k=8)
        for b in range(2):
            act(cact.v(cactv[:, :, 32 * b:32 * b + 1]), cTs.v(cTv[:, :, b:b + 1]), AF.Silu)
        act(arow, rows[:, R_ALOG:R_ALOG + 16], AF.Exp)
        ts(arow, arow, -1.0, None, ALU.mult)
        for j in range(12):
            pp, pa, pb_ = ppair()
            for k in range(8):
                mm(pa[0:33, :], cact.v(cactv[:, k, :]), wada.v(wadav[:, k, j * 512:(j + 1) * 512]), k == 0, k == 7)
            tt(mrow[0:33, :], pa[0:33, :], brow[0:33, j * 512:(j + 1) * 512], ALU.add)
            vec = {0: 0, 1: 0, 2: 1, 3: 1, 4: 4, 5: 4, 6: 2, 7: 2, 8: 3, 9: 3, 10: 5, 11: 5}[j]
            for b in range(2):
                for q in range(4):
                    mm(pb_[:, b * 4 + q:b * 4 + q + 1], mrow[32 * b:32 * b + 1, q * 128:(q + 1) * 128], onesf[32 * b:32 * b + 1, 0:1], True, True)
                col = b * 48 + vec * 8 + (j % 2) * 4
                cp(modv[:, col:col + 4], pb_[:, b * 4:b * 4 + 4])
        for b in range(2):
            for wch in range(2):
                sc_ = modv[:, b * 48 + (1 + 2 * wch) * 8: b * 48 + (1 + 2 * wch) * 8 + 8]
                nw_ = cst[:, (C_NMW if wch == 0 else C_NLW):(C_NMW if wch == 0 else C_NLW) + 8]
                stt(gwv[:, b * 16 + wch * 8:b * 16 + wch * 8 + 8], sc_, 1.0, nw_, ALU.add, ALU.mult)

        def wreset():
            off[0] = work_lo

        cnt = {"ring": 0, "ost": 0}

        def load_piece(src_fn):
            i = cnt["ring"] % NRING
            cnt["ring"] += 1
            slot = ring[i]
            S.dma("pool", "dr%d" % i, src_fn(slot), writes=[slot])
            return slot

        def piece_cols(w_d, c0, ncol):
            def f(slot):
                v = slot.ap[:, 0:8 * ncol].rearrange("p (k c) -> p k c", k=8)
                return [lambda e: e.dma_start(out=v, in_=w_d[:, c0:c0 + ncol].rearrange("(k p) c -> p k c", p=128))]
            return f

        def piece_rows(w_d, r0, nk):
            def f(slot):
                v = slot.ap[:, 0:nk * 1024].rearrange("p (k c) -> p k c", k=nk)
                return [lambda e: e.dma_start(out=v, in_=w_d[r0:r0 + nk * 128, :].rearrange("(k p) c -> p k c", p=128))]
            return f

        def cols_view(slot, ncol):
            return slot.ap[:, 0:8 * ncol].rearrange("p (k c) -> p k c", k=8)

        def rows_view(slot, nk):
            return slot.ap[:, 0:nk * 1024].rearrange("p (k c) -> p k c", k=nk)

        def rmsnorm_to_hT(b, wch, xn_bufs, junk):
            for t in range(8):
                act(junk, Xt[t], AF.Square, accum=ssq[:, t:t + 1])
            act(ssq[:, 8:16], ssq[:, 0:8], AF.Ln, bias=EPS, scale=1.0 / D)
            act(ssq[:, 8:16], ssq[:, 8:16], AF.Exp, scale=-0.5)
            for g in range(4):
                pairs = [ppair(), ppair()]
                for j in range(2):
                    t = 2 * g + j
                    xn = xn_bufs[t % 2]
                    ts(xn, Xt[t], ssq[:, 8 + t:9 + t], None, ALU.mult)
                    for k in range(8):
                        pp = pairs[k // 4][1 + (k % 4) // 2]
                        c0 = ((k % 4) % 2) * 256 + j * 128
                        tr(pp[:, c0:c0 + 128], xn[:, k * 128:(k + 1) * 128], ident)
                for k in range(8):
                    pp = pairs[k // 4][1 + (k % 4) // 2]
                    c0 = ((k % 4) % 2) * 256
                    act(hTs(k, g * 256, 256), pp[:, c0:c0 + 256], AF.Identity,
                        bias=modv[:, b * 48 + (2 * wch) * 8 + k: b * 48 + (2 * wch) * 8 + k + 1],
                        scale=gwv[:, b * 16 + wch * 8 + k: b * 16 + wch * 8 + k + 1])

        def resid_add(t, pp, grp, tmp):
            tt(tmp, pp, grp, ALU.mult)
            tt(Xt[t], Xt[t], tmp, ALU.add)

        for sb in range(nsb):
            b = sb // 2
            first = (sb % 2 == 0)
            tokbase = (sb % 2) * SBT
            for t in range(8):
                S.dma("sp", "dx%d" % t, [(lambda e, t=t, b=b, tokbase=tokbase: e.dma_start(out=Xt[t].ap, in_=x_d[b, tokbase + t * 128: tokbase + (t + 1) * 128, :]))],
                      writes=[Xt[t]])
            xb_p = [load_piece(piece_cols(win_d, 1024 + 512 * i, 512)) for i in range(3)]
            z_p = [load_piece(piece_cols(win_d, 512 * i, 512)) for i in range(2)]
            wos_p = [load_piece(piece_rows(wout_d, 512 * i, 4)) for i in range(2)]

            wreset()
            xn_bufs = [A("xn0", 1024), A("xn1", 1024)]
            junk = A("junk", 1024, BF16)
            rmsnorm_to_hT(b, 0, xn_bufs, junk)
            if dbg and sb == 0:
                S.dma("sp", "ddbg", [lambda e: e.dma_start(out=dbg_d["dbgH"][:, :], in_=hT.ap)], reads=[T(None, hTg)])

            wreset()
            xpre = [A("xpre%d" % i, 260, BF16) for i in range(2)]
            xbcT = [A("xbcT%d" % c, 256, BF16) for c in range(12)]
            xh2 = [A("xh%d" % i, 1024, BF16) for i in range(2)]
            Bt2 = [A("Bt%d" % i, 256, BF16) for i in range(2)]
            esb2 = [A("esb%d" % i, 48) for i in range(2)]
            xw2 = [A("xw%d" % i, 1024, BF16) for i in range(2)]
            yb2 = [A("yb%d" % i, 1024) for i in range(2)]
            dtt = A("dtt", 32)
            lat = A("lat", 32)
            acsb = A("acsb", 48)
            dtw = A("dtw", 16)
            rla = [A("rla%d" % i, 512) for i in range(2)]
            Eb = [A("E%d" % i, 512, BF16) for i in range(2)]
            MT = A("MT", 2048, BF16)
            cbm = A("cbm", 256)
            xr = A("xr", 1024, BF16)
            tF = A("tF", 1024)
            tB = A("tB", 1024)
            ycT = [A("ycT%d" % k, 128, BF16) for k in range(8)]
            ss2 = A("ss2", 4)
            dgt = tF[:, 0:128]
            if first:
                memset(Sst, 0.0)
                memset(Sb, 0.0)
                memset(halo_s, 0.0)
                for c in range(8):
                    memset(T(ucf[c][0][:, 0:30], ucf[c][1]), 0.0)
                for which in range(2):
                    for hk in range(2):
                        pp, pa, pb_ = ppair()
                        for q in range(4):
                            k = hk * 4 + q
                            col = b * 48 + (4 + which) * 8 + k
                            ts(dgt, ident, modv[:, col:col + 1], None, ALU.mult)
                            mm(pa[:, q * 128:(q + 1) * 128], onesf, dgt, True, True)
                        cp(grep[which][:, hk * 512:(hk + 1) * 512], pa[:, :])

            def X_stage(blk):
                tok0 = blk * 256
                pp, pa, pb_ = ppair()
                wdv = wdt.ap.rearrange("p (k c) -> p k c", k=8)
                for j in range(2):
                    for k in range(8):
                        mm(pa[:, j * 16:(j + 1) * 16], hTs(k, tok0 + j * 128, 128), wdt.v(wdv[:, k, :]), k == 0, k == 7)
                for j in range(2):
                    tt(dtt[:, j * 16:(j + 1) * 16], pa[:, j * 16:(j + 1) * 16], rows[:, R_DTB:R_DTB + 16], ALU.add)
                act(dtt, dtt, AF.Exp)
                act(dtt, dtt, AF.Ln, bias=1.0)
                for j in range(2):
                    tt(lat[:, j * 16:(j + 1) * 16], dtt[:, j * 16:(j + 1) * 16], arow, ALU.mult)
                yield

                for c0 in range(0, 12, 2):
                    prs = []
                    for c in (c0, c0 + 1):
                        pp, pa, pb_ = ppair()
                        prs.append((pa, pb_))
                        wv = cols_view(xb_p[c // 4], 512)
                        for k in range(8):
                            mm(pa[:, 0:256], xb_p[c // 4].v(wv[:, k, (c % 4) * 128:(c % 4 + 1) * 128]), hTs(k, tok0, 256), k == 0, k == 7)
                    for c, (pa, pb_) in zip((c0, c0 + 1), prs):
                        xp = xpre[c % 2]
                        cp(xp[:, 0:3], halo_s[:, c * 3:c * 3 + 3])
                        act(xp[:, 3:259], pa[:, 0:256], AF.Copy)
                        cp(halo_s[:, c * 3:c * 3 + 3], xp[:, 256:259])
                    for c, (pa, pb_) in zip((c0, c0 + 1), prs):
                        xp = xpre[c % 2]
                        for k in range(4):
                            for i in range(4):
                                mmq(pb_[32 * i:32 * i + 32, 0:256], dgs.v(dgs3[32 * i:32 * i + 32, c * 4 + k, :]),
                                    xp[32 * i:32 * i + 32, k:k + 256], k == 0, k == 3, i)
                        act(xbcT[c], pb_[:, 0:256], AF.Silu, bias=cst[:, C_CSB + c:C_CSB + c + 1])
                    yield
            def F_stage(g):
                j = g % 2
                g2 = g % 2
                xh, Bt, esb, xw, yb = xh2[g2], Bt2[g2], esb2[g2], xw2[g2], yb2[g2]
                dt_j = dtt[:, j * 16:(j + 1) * 16]
                la_j = lat[:, j * 16:(j + 1) * 16]
                pp, pa, pb_ = ppair()
                pab = pa.v(pa.ap.bitcast(BF16))
                pbb = pb_.v(pb_.ap.bitcast(BF16))
                for c in range(8):
                    tr(pab[:, c * 128:(c + 1) * 128], xbcT[c][:, j * 128:(j + 1) * 128], identb)
                for gg in range(2):
                    tr(pbb[:, gg * 128:(gg + 1) * 128], xbcT[8 + gg][:, j * 128:(j + 1) * 128], identb)
                cp(xh, pab[:, 0:1024])
                cp(Bt, pbb[:, 0:256])
                yield
                pq, pqa, pqb = ppair()
                mm(pqa[:, 0:16], triU, la_j, True, True)
                mm(pqa[:, 16:32], onesf, la_j, True, True)
                for gg in range(2):
                    mm(pqb[:, gg * 128:(gg + 1) * 128], xbcT[8 + gg][:, j * 128:(j + 1) * 128], xbcT[10 + gg][:, j * 128:(j + 1) * 128], True, True)
                act(acsb[:, 0:32], pqa[:, 0:32], AF.Copy)
                tt(acsb[:, 32:48], acsb[:, 16:32], acsb[:, 0:16], ALU.subtract)
                act(esb, acsb, AF.Exp)
                tt(dtw, dt_j, esb[:, 32:48], ALU.mult)
                cbv = cbm.ap.rearrange("p (g l) -> p g l", g=2)
                tt(cbm.v(cbv), pqb.v(pqb.ap[:, 0:256].rearrange("p (g l) -> p g l", g=2)),
                   triU.v(triU.ap.unsqueeze(1).to_broadcast([128, 2, 128])), ALU.mult)
                xh3 = xh.v(xh.ap.rearrange("p (h q) -> p h q", h=16))
                tt(xr.v(xr.ap.rearrange("p (h q) -> p h q", h=16)), xh3, dt_j.v(dt_j.ap.unsqueeze(2).to_broadcast([128, 16, 64])), ALU.mult)
                tt(xw.v(xw.ap.rearrange("p (h q) -> p h q", h=16)), xh3, dtw.v(dtw.ap.unsqueeze(2).to_broadcast([128, 16, 64])), ALU.mult)
                yield
                MTv = MT.ap.rearrange("p (h l) -> p h l", h=16)
                def mk_rla(q4):
                    rl = rla[q4 % 2]
                    rl3 = rl.v(rl.ap.rearrange("p (h l) -> p h l", h=4))
                    la4 = la_j[:, q4 * 4:(q4 + 1) * 4]
                    tt(rl3, triU.v(triU.ap.unsqueeze(1).to_broadcast([128, 4, 128])),
                       la4.v(la4.ap.unsqueeze(2).to_broadcast([128, 4, 128])), ALU.mult, eng="pool")

                mk_rla(0)
                mk_rla(1)
                for q4 in range(4):
                    rl = rla[q4 % 2]
                    Eq = Eb[q4 % 2]
                    pg, pga, pgb = ppair()
                    mm(pga[:, :], mstr, rl, True, True)
                    if q4 < 2:
                        mk_rla(q4 + 2)
                    act(Eq, pga, AF.Exp)
                    gg = q4 // 2
                    tt(MT.v(MTv[:, q4 * 4:(q4 + 1) * 4, :]), Eq.v(Eq.ap.rearrange("p (h l) -> p h l", h=4)),
                       cbm.v(cbv[:, gg:gg + 1, :].to_broadcast([128, 4, 128])), ALU.mult)
                    yield
                tt(tF.v(tF.ap.rearrange("p (h q) -> p h q", h=16)), xh3,
                   rows.v(rows.ap[:, R_DSK:R_DSK + 16].unsqueeze(2).to_broadcast([128, 16, 64])), ALU.mult)
                pd, pda, pdb = ppair()
                xrv = xr.ap.rearrange("p (h q) -> p h q", h=16)
                for h in range(16):
                    pdd = pda if h < 8 else pdb
                    mm(pdd[:, (h % 8) * 64:(h % 8 + 1) * 64], MT.v(MTv[:, h, :]), xr.v(xrv[:, h, :]), True, True)
                tt(yb, pd, tF, ALU.add)
                yield

            def S_stage(g):
                j = g % 2
                g2 = g % 2
                Bt, esb, xw, yb = Bt2[g2], esb2[g2], xw2[g2], yb2[g2]
                po, poa, pob = ppair()
                Sbv = Sb.ap.rearrange("p (g m) -> p g m", g=2)
                for gg, pgg in enumerate((poa, pob)):
                    mm(pgg[:, :], xbcT[10 + gg][:, j * 128:(j + 1) * 128], Sb.v(Sbv[:, gg, :]), True, True)
                tt(tB.v(tB.ap.rearrange("p (h q) -> p h q", h=16)), po.v(po.ap.rearrange("p (h q) -> p h q", h=16)),
                   esb.v(esb.ap[:, 0:16].unsqueeze(2).to_broadcast([128, 16, 64])), ALU.mult)
                tt(yb, yb, tB, ALU.add)
                pst, psa, psb = ppair()
                Btv = Bt.ap.rearrange("p (g n) -> p g n", g=2)
                xwv = xw.ap.rearrange("p (g m) -> p g m", g=2)
                for gg, pgg in enumerate((psa, psb)):
                    mm(pgg[:, :], Bt.v(Btv[:, gg, :]), xw.v(xwv[:, gg, :]), True, True)
                S3 = Sst.v(Sst.ap.rearrange("p (h q) -> p h q", h=16))
                tt(S3, S3, esb.v(esb.ap[:, 16:32].unsqueeze(2).to_broadcast([128, 16, 64])), ALU.mult)
                tt(Sst, Sst, pst, ALU.add)
                act(Sb, Sst, AF.Copy)

            def B_stage(g):
                t = g
                tk = g * 128
                yb = yb2[g % 2]
                zs = tB
                gnb = tB
                pz, pza, pzb = ppair()
                for half, pzz in enumerate((pza, pzb)):
                    zv = cols_view(z_p[half], 512)
                    for k in range(8):
                        mm(pzz[:, :], hTs(k, tk, 128), z_p[half].v(zv[:, k, :]), k == 0, k == 7)
                act(zs, pz, AF.Silu)
                tt(yb, yb, zs, ALU.mult)
                yield
                pj, pja, pjb = ppair()
                for gg, pjj in enumerate((pja, pjb)):
                    act(pjj[:, :], yb[:, gg * 512:(gg + 1) * 512], AF.Square, accum=ss2[:, gg:gg + 1])
                act(ss2[:, 2:4], ss2[:, 0:2], AF.Ln, bias=EPS, scale=1.0 / 512)
                act(ss2[:, 2:4], ss2[:, 2:4], AF.Exp, scale=-0.5)
                for gg in range(2):
                    ts(gnb[:, gg * 512:(gg + 1) * 512], yb[:, gg * 512:(gg + 1) * 512], ss2[:, 2 + gg:3 + gg], None, ALU.mult)
                yield
                pt, pta, ptb = ppair()
                for k in range(8):
                    tr(pt[:, k * 128:(k + 1) * 128], gnb[:, k * 128:(k + 1) * 128], ident)
                for k in range(8):
                    act(ycT[k], pt[:, k * 128:(k + 1) * 128], AF.Copy, scale=cst[:, C_SNW + k:C_SNW + k + 1])
                yield
                px, pxa, pxb = ppair()
                for half, pxx in enumerate((pxa, pxb)):
                    for k in range(8):
                        wv = rows_view(wos_p[k // 4], 4)
                        mm(pxx[:, :], ycT[k], wos_p[k // 4].v(wv[:, k % 4, half * 512:(half + 1) * 512]), k == 0, k == 7)
                resid_add(t, px, grep[0], yb)
                yield

            def chain(*gens):
                for gn in gens:
                    yield from gn

            def merge(ga, na, gb, nb):
                ia = ib = 0
                da = db = False
                while not (da and db):
                    ta = (ia + 1) * nb if not da else None
                    tb_ = (ib + 1) * na if not db else None
                    pick_a = (not da) and (db or ta <= tb_)
                    if pick_a:
                        try:
                            next(ga)
                            ia += 1
                        except StopIteration:
                            da = True
                    else:
                        try:
                            next(gb)
                            ib += 1
                        except StopIteration:
                            db = True

            for _ in X_stage(0):
                pass
            for _ in F_stage(0):
                pass
            def SB_stage(g):
                S_stage(g)
                yield
                yield from B_stage(g)

            p3_pieces = None
            for g in range(8):
                if g == 6:
                    gb0 = load_piece(piece_cols(win_d, 3600, 512))
                    ga0 = load_piece(piece_cols(win_d, 2576, 512))
                    gb1 = load_piece(piece_cols(win_d, 3600 + 512, 512))
                    ga1 = load_piece(piece_cols(win_d, 2576 + 512, 512))
                    p3_pieces = ([ga0, ga1], [gb0, gb1])
                front = []
                nf = 0
                if g % 2 == 1 and g < 7:
                    front.append(X_stage((g + 1) // 2))
                    nf += 7
                if g < 7:
                    front.append(F_stage(g + 1))
                    nf += 8
                if front:
                    merge(SB_stage(g), 5, chain(*front), nf)
                else:
                    for _ in SB_stage(g):
                        pass
            if dbg and sb == 0:
                S.dma("sp", "ddbg", [lambda e: e.dma_start(out=dbg_d["dbgA"].rearrange("(t p) d -> p t d", p=128), in_=Xv)], reads=[X])

            ga_p, gb_p = p3_pieces
            woc_p = [load_piece(piece_rows(wout_d, 1024 + 512 * i, 4)) for i in range(2)]
            wreset()
            sig = [A("sig%d" % i, TB3) for i in range(2)]
            vv = [A("v%d" % c, TB3) for c in range(8)]
            vsq = [A("vsq%d" % i, TB3) for i in range(2)]
            mean_sb = A("mean_sb", TB3)
            rstd_sb = A("rstd_sb", TB3)
            tln = [A("tln%d" % i, TB3) for i in range(2)]
            ucT = [A("ucT%d" % c, TB3, BF16) for c in range(8)]
            tmpx = A("tmpx3", 1024)
            for blk in range(SBT // TB3):
                tok0 = blk * TB3
                psta, pstb = plong
                nrot[0] = 3

                def glu(c):
                    pp, pa, pb_ = ppair()
                    gbv = cols_view(gb_p[c // 4], 512)
                    gav = cols_view(ga_p[c // 4], 512)
                    for k in range(8):
                        mm(pa[:, 0:TB3], gb_p[c // 4].v(gbv[:, k, (c % 4) * 128:(c % 4 + 1) * 128]), hTs(k, tok0, TB3), k == 0, k == 7)
                    for k in range(8):
                        mm(pb_[:, 0:TB3], ga_p[c // 4].v(gav[:, k, (c % 4) * 128:(c % 4 + 1) * 128]), hTs(k, tok0, TB3), k == 0, k == 7)
                    return pa, pb_

                def stats(c):
                    mm(psta[:, 0:TB3], onesD, vv[c], c == 0, c == 7)
                    mm(pstb[:, 0:TB3], onesD, vsq[c % 2], c == 0, c == 7)

                nxt = glu(0)
                for c in range(8):
                    pa, pb_ = nxt
                    sg = sig[c % 2]
                    act(sg, pa[:, 0:TB3], AF.Sigmoid)
                    uap, uh, ub = ucf[c]
                    tt(T(uap[:, 30:30 + TB3], ub), pb_[:, 0:TB3], sg, ALU.mult)
                    if c < 7:
                        nxt = glu(c + 1)
                    if c > 0:
                        stats(c - 1)
                    v = vv[c]
                    pp2, pc, pc2 = ppair()
                    for k in range(31):
                        for i in range(4):
                            mmq(pc[32 * i:32 * i + 32, 0:TB3], dgc.v(dgc3[32 * i:32 * i + 32, c * 31 + k, :]),
                                T(uap[32 * i:32 * i + 32, k:k + TB3], [uh, ub]), k == 0, k == 30, i)
                    act(v, pc[:, 0:TB3], AF.Identity, bias=cst[:, C_CFB + c:C_CFB + c + 1])
                    act(vsq[c % 2], v, AF.Square)
                stats(7)
                for c in range(8):
                    uap, uh, ub = ucf[c]
                    cp(T(uap[:, 0:30], uh), T(uap[:, TB3:TB3 + 30], ub))
                act(mean_sb, psta[:, 0:TB3], AF.Copy)
                tt(rstd_sb, mean_sb, mean_sb, ALU.mult)
                tt(rstd_sb, pstb[:, 0:TB3], rstd_sb, ALU.subtract)
                act(rstd_sb, rstd_sb, AF.Sqrt, bias=EPS)
                recip(rstd_sb, rstd_sb)
                for c in range(8):
                    tl = tln[c % 2]
                    tt(tl, vv[c], mean_sb, ALU.subtract)
                    tt(tl, tl, rstd_sb, ALU.mult)
                    act(ucT[c], tl, AF.Silu, bias=cst[:, C_LNB + c:C_LNB + c + 1], scale=cst[:, C_LNW + c:C_LNW + c + 1])
                for j in range(TB3 // 128):
                    t = blk * (TB3 // 128) + j
                    px, pxa, pxb = ppair()
                    for half, pxx in enumerate((pxa, pxb)):
                        for k in range(8):
                            wv = rows_view(woc_p[k // 4], 4)
                            mm(pxx[:, :], ucT[k][:, j * 128:(j + 1) * 128], woc_p[k // 4].v(wv[:, k % 4, half * 512:(half + 1) * 512]), k == 0, k == 7)
                    resid_add(t, px, grep[0], tmpx)
            if dbg and sb == 0:
                S.dma("sp", "ddbg", [lambda e: e.dma_start(out=dbg_d["dbgB"].rearrange("(t p) d -> p t d", p=128), in_=Xv)], reads=[X])

            nrot[0] = 3
            wreset()
            xn_bufs = [A("xn0b", 1024), A("xn1b", 1024)]
            junk = A("junkb", 1024, BF16)
            rr = [A("rr%d" % i, 512) for i in range(2)]
            hid = [A("hid%d" % f, 512, BF16) for f in range(8)]
            tmpx = A("tmpx4", 1024)
            rmsnorm_to_hT(b, 1, xn_bufs, junk)
            for q4 in range(4):
                w1p = [load_piece(piece_cols(w1_d, 1024 * q4 + 512 * i, 512)) for i in range(2)]
                w2p = [load_piece(piece_rows(w2_d, 1024 * q4 + 512 * i, 4)) for i in range(2)]
                for blk in range(2):
                    for f in range(8):
                        pp, pa, pb_ = ppair()
                        w1v = cols_view(w1p[f // 4], 512)
                        for k in range(8):
                            mm(pa[:, :], w1p[f // 4].v(w1v[:, k, (f % 4) * 128:(f % 4 + 1) * 128]), hTs(k, blk * 512, 512), k == 0, k == 7)
                        r = rr[f % 2]
                        act(r, pa, AF.Relu)
                        tt(hid[f], r, r, ALU.mult)
                    for j in range(4):
                        t = blk * 4 + j
                        px, pxa, pxb = ppair()
                        for half, pxx in enumerate((pxa, pxb)):
                            for f in range(8):
                                w2v = rows_view(w2p[f // 4], 4)
                                mm(pxx[:, :], hid[f][:, j * 128:(j + 1) * 128], w2p[f // 4].v(w2v[:, f % 4, half * 512:(half + 1) * 512]), f == 0, f == 7)
                        resid_add(t, px, grep[1], tmpx)
            if dbg and sb == 0:
                S.dma("sp", "ddbg", [lambda e: e.dma_start(out=dbg_d["dbgC"].rearrange("(t p) d -> p t d", p=128), in_=Xv)], reads=[X])

            wreset()
            junk = A("junkf", 1024, BF16)
            ost = [A("ost%d" % i, 1024) for i in range(2)]
            wfin = A("wfin", 1024)
            S.dma("sp", "dwf", [lambda e: e.dma_start(out=wfin.ap, in_=rows_d[:, R_WFIN:R_WFIN + 1024])], writes=[wfin])
            for t in range(8):
                act(junk, Xt[t], AF.Square, accum=ssq[:, t:t + 1])
            act(ssq[:, 8:16], ssq[:, 0:8], AF.Ln, bias=EPS, scale=1.0 / D)
            act(ssq[:, 8:16], ssq[:, 8:16], AF.Exp, scale=-0.5)
            for t in range(8):
                i = cnt["ost"] % 2
                cnt["ost"] += 1
                stt(ost[i], Xt[t], ssq[:, 8 + t:9 + t], wfin, ALU.mult, ALU.mult)
                S.dma("sp", "dout%d" % i, [(lambda e, t=t, i=i, b=b, tokbase=tokbase, o=ost[i]: e.dma_start(out=out_d[b, tokbase + t * 128: tokbase + (t + 1) * 128, :], in_=o.ap))],
                      reads=[ost[i]])

        fin = S.op("sp", lambda e: e.nop())
        for nm in ("dma:dout0", "dma:dout1", "dma:ddbg"):
            if nm in S.last_dma:
                fin.deps.append(S.last_dma[nm])
        S.plan()
        S.emit(block, sems)
    return nc


_NC_CACHE = {}


def kernel(**inputs):
    inp = {k: np.asarray(v) for k, v in inputs.items()}
    if "nc" not in _NC_CACHE:
        _NC_CACHE["nc"] = build_nc()
    nc = _NC_CACHE["nc"]
    cst, rows = pack_consts(inp)
    shared = {
        "w_ada": np.ascontiguousarray(inp["w_ada"][0]), "b_ada": np.ascontiguousarray(inp["b_ada"][0:1]),
        "w_in": np.ascontiguousarray(inp["w_in"][0]), "w_out": np.ascontiguousarray(inp["w_out"][0]),
        "w_mlp1": np.ascontiguousarray(inp["w_mlp1"][0]), "w_mlp2": np.ascontiguousarray(inp["w_mlp2"][0]),
        "cst": cst, "rows": rows,
    }
    in_maps = []
    for i in range(NCORES):
        m = dict(shared)
        m["x"] = np.ascontiguousarray(inp["x"][2 * i:2 * i + 2])
        m["cT"] = np.ascontiguousarray(inp["c"][2 * i:2 * i + 2].T)
        in_maps.append(m)
    res = run_bass_kernel_spmd(nc, in_maps, core_ids=list(range(NCORES)))
    return np.concatenate([r["out"] for r in res.results], axis=0).astype(np.float32)
```
